# Optimizing a Trainium2 kernel written in Bass

```python
import jax, jax.numpy as jnp
from jax import lax
import numpy as np

D_MODEL = 2048
BATCH = 1
SEQ = 8192
DEPTH = 4
DEC_BATCH = 16
DEC_SEQ = 16
PAST_LEN = 2048

CHUNK = 64
N_A = DEPTH // 2
N_B = DEPTH - N_A
CONV_WIDTH = 31
CONV_PAD = CONV_WIDTH - 1
N_HEADS = 32
N_KV_HEADS = 4
HEAD_DIM = 64
GROUP = N_HEADS // N_KV_HEADS
WINDOW = 128
WIN_CHUNKS = WINDOW // CHUNK
D_FF = ((-(-8 * D_MODEL // 3) + 255) // 256) * 256
ROPE_THETA = 10000.0
EPS = 1e-6
SCALE = HEAD_DIM ** -0.5

kernel_name = 'streaming_conformer_conv_yoco_swa_sink'


def rms_norm(x, g):
    xf = x.astype(jnp.float32)
    y = xf * lax.rsqrt(jnp.mean(xf * xf, axis=-1, keepdims=True) + EPS)
    return (y * g.astype(jnp.float32)).astype(x.dtype)


def layer_norm(x, g, b):
    xf = x.astype(jnp.float32)
    xc = xf - jnp.mean(xf, axis=-1, keepdims=True)
    var = jnp.mean(xc * xc, axis=-1, keepdims=True)
    y = xc * lax.rsqrt(var + EPS) * g.astype(jnp.float32) + b.astype(jnp.float32)
    return y.astype(x.dtype)


def rope(x, pos):
    half = HEAD_DIM // 2
    inv_freq = ROPE_THETA ** (-jnp.arange(half, dtype=jnp.float32) / half)
    ang = pos.astype(jnp.float32)[:, None] * inv_freq[None, :]
    cos = jnp.cos(ang)[:, None, :]
    sin = jnp.sin(ang)[:, None, :]
    xf = x.astype(jnp.float32)
    x1, x2 = xf[..., :half], xf[..., half:]
    return jnp.concatenate([x1 * cos - x2 * sin, x2 * cos + x1 * sin], axis=-1).astype(x.dtype)


def heads(h, w, g, n, pos):
    b, t = h.shape[:2]
    z = jnp.matmul(h, w).reshape(b, t, n, HEAD_DIM)
    return rope(rms_norm(z, g), pos)


def shared_kv(s, kv_norm, w_k, w_v, k_norm, pos):
    h = rms_norm(s, kv_norm)
    b, t = s.shape[:2]
    k = heads(h, w_k, k_norm, N_KV_HEADS, pos)
    v = jnp.matmul(h, w_v).reshape(b, t, N_KV_HEADS, HEAD_DIM)
    return k, v


def swiglu(x, g, w_gate, w_up, w_down):
    h = rms_norm(x, g)
    return jnp.matmul(jax.nn.silu(jnp.matmul(h, w_gate)) * jnp.matmul(h, w_up), w_down)


def conv_module(x, ctx, norm_g, w_pw1, b_pw1, w_dw, b_dw, ln_g, ln_b, w_pw2, b_pw2):
    h = rms_norm(x, norm_g)
    a = jnp.matmul(h, w_pw1) + b_pw1
    u = a[..., :D_MODEL] * jax.nn.sigmoid(a[..., D_MODEL:])
    up = jnp.concatenate([ctx.astype(u.dtype), u], axis=1)
    c = lax.conv_general_dilated(up, w_dw[:, None, :], window_strides=(1,), padding='VALID',
                                 dimension_numbers=('NWC', 'WIO', 'NWC'),
                                 feature_group_count=D_MODEL) + b_dw
    c = jax.nn.silu(layer_norm(c, ln_g, ln_b))
    return jnp.matmul(c, w_pw2) + b_pw2, up[:, -CONV_PAD:]


def sink_weights(s, sink):
    m = jnp.maximum(jnp.max(s, axis=-1, keepdims=True), sink)
    p = jnp.exp(s - m)
    return p / (jnp.sum(p, axis=-1, keepdims=True) + jnp.exp(sink - m))


def band_attention(q, k, v, sinks):
    b, t = q.shape[:2]
    nc = t // CHUNK
    lead = WIN_CHUNKS * CHUNK
    qb = q.reshape(b, nc, CHUNK, N_KV_HEADS, GROUP, HEAD_DIM)
    pad = ((0, 0), (lead, 0), (0, 0), (0, 0))
    kp = jnp.pad(k, pad).reshape(b, nc + WIN_CHUNKS, CHUNK, N_KV_HEADS, HEAD_DIM)
    vp = jnp.pad(v, pad).reshape(b, nc + WIN_CHUNKS, CHUNK, N_KV_HEADS, HEAD_DIM)
    kb = jnp.concatenate([kp[:, j:j + nc] for j in range(WIN_CHUNKS + 1)], axis=2)
    vb = jnp.concatenate([vp[:, j:j + nc] for j in range(WIN_CHUNKS + 1)], axis=2)
    key_chunk = jnp.arange(nc)[:, None] + jnp.repeat(jnp.arange(WIN_CHUNKS + 1) - WIN_CHUNKS, CHUNK)[None, :]
    valid = key_chunk >= 0
    s = jnp.einsum('bcqkgd,bcskd->bckgqs', qb, kb, preferred_element_type=jnp.float32) * SCALE
    s = jnp.where(valid[None, :, None, None, None, :], s, -jnp.inf)
    sink = sinks.astype(jnp.float32).reshape(N_KV_HEADS, GROUP)[None, None, :, :, None, None]
    p = sink_weights(s, sink).astype(v.dtype)
    o = jnp.einsum('bckgqs,bcskd->bcqkgd', p, vb)
    return o.reshape(b, t, N_HEADS * HEAD_DIM)


def window_attention_step(q, k_all, v_all, sinks):
    b, t = q.shape[:2]
    qg = q.reshape(b, t, N_KV_HEADS, GROUP, HEAD_DIM)
    s = jnp.einsum('btkgd,bskd->bkgts', qg, k_all, preferred_element_type=jnp.float32) * SCALE
    sink = sinks.astype(jnp.float32).reshape(N_KV_HEADS, GROUP)[None, :, :, None, None]
    p = sink_weights(s, sink).astype(v_all.dtype)
    o = jnp.einsum('bkgts,bskd->btkgd', p, v_all)
    return o.reshape(b, t, N_HEADS * HEAD_DIM)


def setup_inputs(seed: int = 0) -> dict:
    key = jax.random.key(seed)
    ks = iter(jax.random.split(key, 32))

    def nrm(shape, scale=1.0):
        return jax.random.normal(next(ks), shape, jnp.float32) * scale

    def gain(shape):
        return 1.0 + 0.01 * nrm(shape)

    D, F = D_MODEL, D_FF
    HQ, HKV = N_HEADS * HEAD_DIM, N_KV_HEADS * HEAD_DIM
    rows = min(WINDOW, PAST_LEN)
    return {
        'x_prompt': nrm((BATCH, SEQ, D)),
        'x_sample': nrm((DEC_BATCH, DEC_SEQ, D)),
        'state_conv': nrm((N_A, DEC_BATCH, CONV_PAD, D), 0.5),
        'cache_k': nrm((DEC_BATCH, rows, N_KV_HEADS, HEAD_DIM)),
        'cache_v': nrm((DEC_BATCH, rows, N_KV_HEADS, HEAD_DIM)),
        'ffn_norm': gain((DEPTH, D)),
        'w_gate': nrm((DEPTH, D, F), D ** -0.5),
        'w_up': nrm((DEPTH, D, F), D ** -0.5),
        'w_down': nrm((DEPTH, F, D), F ** -0.5),
        'conv_norm': gain((N_A, D)),
        'w_pw1': nrm((N_A, D, 2 * D), D ** -0.5),
        'b_pw1': nrm((N_A, 2 * D), 0.01),
        'w_dw': nrm((N_A, CONV_WIDTH, D), CONV_WIDTH ** -0.5),
        'b_dw': nrm((N_A, D), 0.01),
        'conv_ln_g': gain((N_A, D)),
        'conv_ln_b': nrm((N_A, D), 0.01),
        'w_pw2': nrm((N_A, D, D), D ** -0.5),
        'b_pw2': nrm((N_A, D), 0.01),
        'kv_norm': gain((D,)),
        'w_k': nrm((D, HKV), D ** -0.5),
        'w_v': nrm((D, HKV), D ** -0.5),
        'k_norm': gain((HEAD_DIM,)),
        'attn_norm': gain((N_B, D)),
        'w_q': nrm((N_B, D, HQ), D ** -0.5),
        'q_norm': gain((N_B, HEAD_DIM)),
        'sinks': nrm((N_B, N_HEADS), 0.5),
        'w_o': nrm((N_B, HQ, D), HQ ** -0.5),
    }


def reference(x_prompt, x_sample, state_conv, cache_k, cache_v,
              ffn_norm, w_gate, w_up, w_down,
              conv_norm, w_pw1, b_pw1, w_dw, b_dw, conv_ln_g, conv_ln_b, w_pw2, b_pw2,
              kv_norm, w_k, w_v, k_norm,
              attn_norm, w_q, q_norm, sinks, w_o):
    pos_p = jnp.arange(x_prompt.shape[1])
    pos_s = PAST_LEN + jnp.arange(x_sample.shape[1])
    xp, xs = x_prompt, x_sample
    ctx_p = jnp.zeros((xp.shape[0], CONV_PAD, D_MODEL), xp.dtype)
    conv_p, conv_s = [], []
    for i in range(DEPTH):
        if i < N_A:
            cargs = (conv_norm[i], w_pw1[i], b_pw1[i], w_dw[i], b_dw[i],
                     conv_ln_g[i], conv_ln_b[i], w_pw2[i], b_pw2[i])
            out_p, st_p = conv_module(xp, ctx_p, *cargs)
            out_s, st_s = conv_module(xs, state_conv[i], *cargs)
            conv_p.append(st_p)
            conv_s.append(st_s)
        else:
            j = i - N_A
            qp = heads(rms_norm(xp, attn_norm[j]), w_q[j], q_norm[j], N_HEADS, pos_p)
            qs = heads(rms_norm(xs, attn_norm[j]), w_q[j], q_norm[j], N_HEADS, pos_s)
            out_p = jnp.matmul(band_attention(qp, k_p, v_p, sinks[j]), w_o[j])
            out_s = jnp.matmul(window_attention_step(qs, k_all, v_all, sinks[j]), w_o[j])
        xp = xp + out_p
        xs = xs + out_s
        xp = xp + swiglu(xp, ffn_norm[i], w_gate[i], w_up[i], w_down[i])
        xs = xs + swiglu(xs, ffn_norm[i], w_gate[i], w_up[i], w_down[i])
        if i == N_A - 1:
            k_p, v_p = shared_kv(xp, kv_norm, w_k, w_v, k_norm, pos_p)
            k_s, v_s = shared_kv(xs, kv_norm, w_k, w_v, k_norm, pos_s)
            k_all = jnp.concatenate([cache_k.astype(k_s.dtype), k_s], axis=1)
            v_all = jnp.concatenate([cache_v.astype(v_s.dtype), v_s], axis=1)
    rows = cache_k.shape[1]
    return (xp, xs, jnp.stack(conv_p), k_p[:, -WINDOW:], v_p[:, -WINDOW:],
            jnp.stack(conv_s), k_all[:, -rows:], v_all[:, -rows:])
```

```python
import contextlib
import numpy as np
import concourse.bass as bass
import concourse.mybir as mybir
from concourse.bass_utils import run_bass_kernel_spmd

F32 = mybir.dt.float32
BF16 = mybir.dt.bfloat16
ALU = mybir.AluOpType
AF = mybir.ActivationFunctionType

D = 2048
KC = 16
FC = 44
NCORE = 8
HALO = 192
OWN = 512
W0 = HALO + OWN
W1 = OWN + 32
WMAX = W0
NS = 5
PARTS = [list(range(0, 15)), list(range(15, 30)), list(range(30, 44))]
EPS = 1e-6
PAST_LEN = 2048
SAME_ENGINE_SYNC = True

_cc = {}
_ncol = 0


def _alloc_cols(name, n):
    global _ncol
    _cc[name] = _ncol
    _ncol += n


for _i in range(2):
    _alloc_cols(("conv_norm", _i), 16)
    _alloc_cols(("b_pw1a", _i), 16)
    _alloc_cols(("b_pw1g", _i), 16)
    _alloc_cols(("w_dw", _i), 31 * 16)
    _alloc_cols(("b_dw", _i), 16)
    _alloc_cols(("ln_g", _i), 16)
    _alloc_cols(("ln_b", _i), 16)
    _alloc_cols(("b_pw2", _i), 16)
for _l in range(4):
    _alloc_cols(("ffn_norm", _l), 16)
_alloc_cols("kv_norm", 16)
for _j in range(2):
    _alloc_cols(("attn_norm", _j), 16)
_alloc_cols("k_norm", 1)
_alloc_cols(("q_norm", 0), 1)
_alloc_cols(("q_norm", 1), 1)
_alloc_cols("halo_mask", 1)
_alloc_cols("halo_bias", 1)
_alloc_cols("eps", 1)
_alloc_cols("zero", 1)
_alloc_cols("pad", 9)
NCONST = _ncol


ALL_STAGES = ("conv0", "ffn0", "conv1", "ffn1", "kv", "attn0", "attn1")


def unit_seq(stages=ALL_STAGES):
    seq = []

    def ffn_units(l):
        for p, part in enumerate(PARTS):
            for f in part:
                seq.append(("wg", l, f))
                seq.append(("wu", l, f))
            for dch in range(16):
                seq.append(("wd", l, p, dch))

    for i in range(2):
        if "conv%d" % i in stages:
            for j in range(16):
                seq.append(("pw1", i, j))
                seq.append(("pw1", i, 16 + j))
            for dch in range(16):
                seq.append(("pw2", i, dch))
        if "ffn%d" % i in stages:
            ffn_units(i)
        if i == 1 and ("kv" in stages or "kvk" in stages):
            seq += [("wk", 0, 0), ("wk", 0, 1), ("wk", 1, 0), ("wk", 1, 1)]
        if i == 1 and ("kv" in stages or "kvv" in stages):
            seq += [("wv", 0), ("wv", 1)]
    for jb in range(2):
        if "attn%d" % jb in stages:
            for j in range(16):
                seq.append(("wq", jb, j))
            for dch in range(16):
                seq.append(("wo", jb, dch))
            if "noffn" not in stages:
                ffn_units(2 + jb)
    return seq


SEQ = unit_seq()
NU = len(SEQ)
_PLAN = {"stages": ALL_STAGES, "passes": (0, 1)}


class Sem:
    def __init__(self, name):
        self.name = name
        self.h = None
        self.count = 0


class Stream:
    def __init__(self, name, is_pe=False):
        self.name = name
        self.sem = Sem("s_" + name)
        self.ops = []
        self.known = {}
        self.is_pe = is_pe

    def need(self, sem, val):
        if sem is self.sem and (self.is_pe or not SAME_ENGINE_SYNC):
            return
        if self.known.get(sem, 0) >= val:
            return
        self.known[sem] = val
        self.ops.append(("wait", sem, val))


class Tracker:
    def __init__(self):
        self.w = {}
        self.r = {}

    def deps(self, reads, writes):
        out = []
        for (b, i, c0, c1) in reads:
            for (a0, a1, dep) in self.w.get((b, i), ()):
                if a0 < c1 and c0 < a1:
                    out.append(dep)
        for (b, i, c0, c1) in writes:
            for (a0, a1, dep) in self.w.get((b, i), ()):
                if a0 < c1 and c0 < a1:
                    out.append(dep)
            for (a0, a1, dep) in self.r.get((b, i), ()):
                if a0 < c1 and c0 < a1:
                    out.append(dep)
        return out

    def commit(self, reads, writes, dep):
        for (b, i, c0, c1) in writes:
            key = (b, i)
            self.w[key] = [x for x in self.w.get(key, ()) if not (c0 <= x[0] and x[1] <= c1)] + [(c0, c1, dep)]
            if key in self.r:
                self.r[key] = [x for x in self.r[key] if not (c0 <= x[0] and x[1] <= c1)]
        for (b, i, c0, c1) in reads:
            key = (b, i)
            lst = self.r.get(key, [])
            lst = [x for x in lst if not (x[2][0] is dep[0] and c0 <= x[0] and x[1] <= c1)]
            lst.append((c0, c1, dep))
            self.r[key] = lst


class Gen:
    def __init__(self, nc):
        self.nc = nc
        self.tr = Tracker()
        self.PE = Stream("pe", is_pe=True)
        self.ACT = Stream("act")
        self.DVE = Stream("dve")
        self.POOL = Stream("pool")
        self.SP = Stream("sp")
        self.sems = [self.PE.sem, self.ACT.sem, self.DVE.sem, self.POOL.sem, self.SP.sem]
        self.ui = 0
        self.tog = 0
        self.out_sems = []

    def new_sem(self, name):
        s = Sem(name)
        self.sems.append(s)
        return s

    def op(self, st, method, args, kw=None, reads=(), writes=(), inc=True):
        kw = kw or {}
        for (s, v) in self.tr.deps(reads, writes):
            st.need(s, v)
        if inc:
            st.sem.count += 1
            dep = (st.sem, st.sem.count)
            st.ops.append(("op", method, args, kw, st.sem, 1))
        else:
            dep = (st.sem, st.sem.count + 1)
            st.ops.append(("op", method, args, kw, None, 0))
        self.tr.commit(reads, writes, dep)

    def dma(self, st, out, in_, sem, reads=(), writes=()):
        for (s, v) in self.tr.deps(reads, writes):
            st.need(s, v)
        sem.count += 16
        st.ops.append(("op", "dma_start", (), {"out": out, "in_": in_}, sem, 16))
        self.tr.commit(reads, writes, (sem, sem.count))

    def mm(self, out, lhsT, rhs, start, stop, reads, writes, inc, tp=None):
        kw = {"start": start, "stop": stop}
        if tp is not None:
            kw["tile_position"] = tp
        self.op(self.PE, "matmul", (out, lhsT, rhs), kw, reads, writes, inc)


def _check_deadlock(streams):
    val = {}
    pc = [0] * len(streams)
    progress = True
    while progress:
        progress = False
        for si, st in enumerate(streams):
            ops = st.ops
            i = pc[si]
            while i < len(ops):
                o = ops[i]
                if o[0] == "wait":
                    if val.get(o[1], 0) < o[2]:
                        break
                else:
                    if o[4] is not None:
                        val[o[4]] = val.get(o[4], 0) + o[5]
                i += 1
            if i != pc[si]:
                progress = True
                pc[si] = i
    stuck = [(st.name, pc[si], st.ops[pc[si]][1].name, st.ops[pc[si]][2], val.get(st.ops[pc[si]][1], 0))
             for si, st in enumerate(streams) if pc[si] < len(st.ops)]
    if stuck:
        raise RuntimeError("semaphore deadlock: %r" % (stuck,))


def R(buf, idx, c0, c1):
    return (buf, idx, c0, c1)


def build_program(stages=ALL_STAGES, passes=(0, 1)):
    global SEQ, NU
    SEQ = unit_seq(stages)
    NU = len(SEQ)
    nc = bass.Bass("TRN2", target_bir_lowering=False)
    g = Gen(nc)

    def din(name, shape):
        return nc.dram_tensor(name, list(shape), F32, kind="ExternalInput").ap()

    def dout(name, shape):
        return nc.dram_tensor(name, list(shape), F32, kind="ExternalOutput").ap()

    xin = [din("xin0", [128, KC, W0]), din("xin1", [128, KC, W1])]
    ropein = [din("rope0", [128, 2, W0]), din("rope1", [128, 2, W1])]
    cin = din("consts", [128, NCONST])
    sinkin = din("sinks", [128, 64])
    permin = din("perm", [128, 128])
    scvin = din("scv", [128, 2, KC, 2, 30])
    cktin = din("ckT", [128, 4, 2, 128])
    cvdin = din("cvd", [128, 2, 4, 128])
    ckraw = din("ck_raw", [2, 128, 256])
    cvraw = din("cv_raw", [2, 128, 256])
    wall = din("wall", [NU, 128, KC, 128])

    o_y = [dout("o_y0", [128, KC, OWN]), dout("o_y1", [128, KC, W1])]
    o_st = dout("o_st", [128, 2, KC, 3, 30])
    o_k = dout("o_k", [128, 2, 160])
    o_v = dout("o_v", [128, 256])
    o_vs = dout("o_vs", [16, 2, 256])
    o_kold = dout("o_kold", [2, 112, 256])
    o_vold = dout("o_vold", [2, 112, 256])

    dbg_out = {}
    if "dbg" in stages:
        for nm, shp in (("o_dbg_a", [128, 704]), ("o_dbg_b", [128, 704]), ("o_dbg_den0", [128, 512]), ("o_dbg_den1", [128, 512])):
            dbg_out[nm] = nc.dram_tensor(nm, shp, F32, kind="ExternalOutput").ap()

    with contextlib.ExitStack() as es:
        def sb(name, shape, dt):
            return es.enter_context(nc.sbuf_tensor(name, list(shape), dt))

        xT = sb("xT", [128, KC, WMAX], F32)
        hT = sb("hT", [128, KC, WMAX], BF16)
        b2 = sb("b2", [128, KC, WMAX], BF16)
        wr = [sb("wr%d" % i, [128, KC, 128], BF16) for i in range(NS)]
        KT = sb("KT", [128, 4, 1152], BF16)
        KTs = sb("KTs", [128, 4, 2, 144], BF16)
        Vd = sb("Vd", [128, 9, 4, 128], BF16)
        Vds = sb("Vds", [128, 2, 4, 128], BF16)
        Vn = sb("Vn", [128, 2, 4, 128], BF16)
        rope = sb("rope", [128, 2, WMAX], F32)
        cst = sb("cst", [128, NCONST], F32)
        snk = sb("snk", [128, 64], F32)
        SE = sb("SE", [128, 64], F32)
        permf = sb("permf", [128, 128], F32)
        perm = sb("perm_b", [128, 128], BF16)
        onesD = sb("onesD", [128, 128], BF16)
        blk64 = sb("blk64", [128, 128], BF16)
        ones1 = sb("ones1", [128, 128], BF16)
        scv = sb("scv_b", [128, 2, KC, 2, 30], BF16)
        carry = sb("carry", [128, 2, KC, 30], BF16)
        UT = [sb("UT%d" % i, [128, 734], BF16) for i in range(2)]
        ctmp = [sb("ctmp%d" % i, [128, 704], F32) for i in range(2)]
        sqt = [sb("sqt%d" % i, [128, 352], BF16) for i in range(4)]
        rs = [sb("rs%d" % i, [128, 352], F32) for i in range(2)]
        ev = [sb("ev%d" % i, [128, 352], F32) for i in range(4)]
        ytmp = [sb("ytmp%d" % i, [128, 352], F32) for i in range(2)]
        ybt = [sb("ybt%d" % i, [128, 352], BF16) for i in range(2)]
        fin = [sb("fin%d" % i, [128, 352], F32) for i in range(2)]
        lnm, lnr, lnn = ytmp, fin, rs
        PT = [[sb("PT%d_%d" % (i, j), [128, 512], BF16) for j in range(2)] for i in range(2)]
        den = [sb("den%d" % i, [128, 512], F32) for i in range(2)]
        STG = sb("STG", [128, 2, KC, 3, 30], BF16)
        KOUT = sb("KOUT", [128, 2, 160], F32)
        VOUT = sb("VOUT", [128, 256], F32)
        VSOUT = sb("VSOUT", [16, 2, 256], F32)
        ps = [es.enter_context(nc.psum_tensor("ps%d" % i, [128, 512], F32)) for i in range(8)]

        PE, ACT, DVE, POOL, SP = g.PE, g.ACT, g.DVE, g.POOL, g.SP
        wsem = [g.new_sem("w%d" % i) for i in range(NS)]
        s_cst = g.new_sem("cst")
        s_x = g.new_sem("x")
        s_rope = g.new_sem("rope")
        s_kv = g.new_sem("kvin")

        def ccol(name, k=0):
            c = _cc[name] + k
            return cst[:, c:c + 1]

        g.dma(SP, cst[:], cin, s_cst, writes=[R("cst", 0, 0, 1)])
        g.dma(SP, snk[:], sinkin, s_cst, writes=[R("snk", 0, 0, 1)])
        g.dma(SP, permf[:], permin, s_cst, writes=[R("permf", 0, 0, 1)])
        tot = (s_cst, s_cst.count)
        for key in (("cst", 0), ("snk", 0), ("permf", 0)):
            g.tr.w[key] = [(0, 1, tot)]
        g.dma(POOL, scv[:], scvin, s_kv, writes=[R("scv", 0, 0, 1)])
        g.dma(POOL, KTs[:, :, :, 0:128], cktin, s_kv, writes=[R("KTs", 0, 0, 128)])
        g.dma(POOL, Vds[:], cvdin, s_kv, writes=[R("Vds", 0, 0, 1)])
        tot = (s_kv, s_kv.count)
        g.tr.w[("scv", 0)] = [(0, 1, tot)]
        g.tr.w[("KTs", 0)] = [(0, 128, tot)]
        g.tr.w[("Vds", 0)] = [(0, 1, tot)]

        g.op(DVE, "memset", (Vn[:].rearrange("p a b c -> p (a b c)"), 0.0), writes=[R("Vn", 0, 0, 1)])
        g.op(DVE, "memset", (onesD[:], 1.0 / D), writes=[R("onesD", 0, 0, 1)])
        g.op(DVE, "memset", (ones1[:], 1.0), writes=[R("ones1", 0, 0, 1)])
        g.op(DVE, "memset", (blk64[:], 0.0), writes=[R("blk64", 0, 0, 1)])
        g.op(DVE, "memset", (blk64[0:64, 0:64], 1.0 / 64), writes=[R("blk64", 0, 0, 1)])
        g.op(DVE, "memset", (blk64[64:128, 64:128], 1.0 / 64), writes=[R("blk64", 0, 0, 1)])
        g.op(DVE, "tensor_copy", (perm[:], permf[:]), reads=[R("permf", 0, 0, 1)], writes=[R("perm", 0, 0, 1)])
        g.op(ACT, "activation", (SE[:], snk[:], AF.Exp), reads=[R("snk", 0, 0, 1)], writes=[R("SE", 0, 0, 1)])

        for nm, tl in (("STG", STG), ("KOUT", KOUT), ("VOUT", VOUT), ("VSOUT", VSOUT)):
            g.op(DVE, "memset", (tl[:], 0.0), writes=[R(nm, 0, 0, 1)] + ([R(nm, 1, 0, 1)] if nm == "STG" else []))

        s_old = g.new_sem("old")
        g.dma(SP, o_kold, ckraw[:, 16:128, :], s_old)
        g.dma(SP, o_vold, cvraw[:, 16:128, :], s_old)
        g.out_sems.append(s_old)

        def get_unit(desc):
            u = g.ui
            assert SEQ[u % NU] == desc, (SEQ[u % NU], desc)
            g.ui += 1
            slot = u % NS
            g.dma(POOL, wr[slot][:], wall[u % NU], wsem[slot], writes=[R("W", slot, 0, 1)])
            return slot

        def toggle():
            g.tog ^= 1
            return g.tog

        def rmsnorm(gname, tiles):
            for ti, (c0, c1) in enumerate(tiles):
                n = c1 - c0
                bank = 6 + (ti % 2)
                for k in range(KC):
                    sq = sqt[k % 4]
                    g.op(ACT, "activation", (sq[:, 0:n], xT[:, k, c0:c1], AF.Square),
                         reads=[R("x", k, c0, c1)], writes=[R("sqt", k % 4, 0, n)])
                    g.mm(ps[bank][:, 0:n], onesD[:], sq[:, 0:n], k == 0, k == KC - 1,
                         reads=[R("sqt", k % 4, 0, n), R("onesD", 0, 0, 1)], writes=[R("ps", bank, 0, n)],
                         inc=True)
                r = rs[ti % 2]
                g.op(ACT, "activation", (r[:, 0:n], ps[bank][:, 0:n], AF.Sqrt),
                     {"bias": ccol("eps")}, reads=[R("ps", bank, 0, n), R("cst", 0, 0, 1)],
                     writes=[R("rs", ti % 2, 0, n)])
                g.op(DVE, "reciprocal", (r[:, 0:n], r[:, 0:n]), reads=[R("rs", ti % 2, 0, n)],
                     writes=[R("rs", ti % 2, 0, n)])
                for k in range(KC):
                    g.op(DVE, "scalar_tensor_tensor",
                         (hT[:, k, c0:c1], xT[:, k, c0:c1], ccol(gname, k), r[:, 0:n], ALU.mult, ALU.mult),
                         reads=[R("x", k, c0, c1), R("rs", ti % 2, 0, n)], writes=[R("h", k, c0, c1)])

        def proj(desc, src, srcname, nk, tiles, banks):
            slot = get_unit(desc)
            for k in range(nk):
                for ti, (c0, c1) in enumerate(tiles):
                    n = c1 - c0
                    g.mm(ps[banks[ti]][:, 0:n], wr[slot][:, k, :], src[:, k, c0:c1], k == 0, k == nk - 1,
                         reads=[R("W", slot, 0, 1), R(srcname, k, c0, c1)], writes=[R("ps", banks[ti], 0, n)],
                         inc=(k == nk - 1))

        def ffn(l, tiles):
            rmsnorm(("ffn_norm", l), tiles)
            for p, part in enumerate(PARTS):
                for fl, f in enumerate(part):
                    proj(("wg", l, f), hT, "h", KC, tiles, (0, 1))
                    proj(("wu", l, f), hT, "h", KC, tiles, (2, 3))
                    for ti, (c0, c1) in enumerate(tiles):
                        n = c1 - c0
                        e = ev[ti]
                        g.op(ACT, "activation", (e[:, 0:n], ps[ti][:, 0:n], AF.Silu),
                             reads=[R("ps", ti, 0, n)], writes=[R("ev", ti, 0, n)])
                        g.op(DVE, "tensor_tensor", (b2[:, fl, c0:c1], ps[2 + ti][:, 0:n], e[:, 0:n], ALU.mult),
                             reads=[R("ps", 2 + ti, 0, n), R("ev", ti, 0, n)], writes=[R("b2", fl, c0, c1)])
                for dch in range(KC):
                    t = toggle()
                    banks = (4 + 2 * t, 5 + 2 * t)
                    proj(("wd", l, p, dch), b2, "b2", len(part), tiles, banks)
                    for ti, (c0, c1) in enumerate(tiles):
                        n = c1 - c0
                        g.op(DVE, "tensor_tensor", (xT[:, dch, c0:c1], ps[banks[ti]][:, 0:n], xT[:, dch, c0:c1], ALU.add),
                             reads=[R("ps", banks[ti], 0, n), R("x", dch, c0, c1)], writes=[R("x", dch, c0, c1)])

        def conv_layer(i, pa, tiles):
            W = W0 if pa == 0 else W1
            if pa == 0:
                segs = [(0, W0, 30)]
                L = W0
            else:
                segs = [(0, 512, 30), (512, 528, 572), (528, 544, 618)]
                L = 604

            def split(c0, c1):
                out = []
                for (a0, a1, u0) in segs:
                    lo, hi = max(a0, c0), min(a1, c1)
                    if lo < hi:
                        out.append((lo, hi, u0 + (lo - a0)))
                return out

            rmsnorm(("conv_norm", i), tiles)
            wdw = _cc[("w_dw", i)]
            for j in range(KC):
                ub = j % 2
                U = UT[ub]
                if pa == 0:
                    g.op(DVE, "memset", (U[:, 0:30], 0.0), writes=[R("UT", ub, 0, 30)])
                else:
                    g.op(ACT, "activation", (U[:, 0:30], carry[:, i, j, :], AF.Copy),
                         reads=[R("carry", i * KC + j, 0, 30)], writes=[R("UT", ub, 0, 30)])
                    for s in range(2):
                        u0 = 542 + 46 * s
                        g.op(ACT, "activation", (U[:, u0:u0 + 30], scv[:, i, j, s, :], AF.Copy),
                             reads=[R("scv", 0, 0, 1)], writes=[R("UT", ub, u0, u0 + 30)])
                proj(("pw1", i, j), hT, "h", KC, tiles, (0, 1))
                proj(("pw1", i, 16 + j), hT, "h", KC, tiles, (2, 3))
                for ti, (c0, c1) in enumerate(tiles):
                    n = c1 - c0
                    e = ev[ti]
                    g.op(ACT, "activation", (e[:, 0:n], ps[2 + ti][:, 0:n], AF.Sigmoid),
                         {"bias": ccol(("b_pw1g", i), j)},
                         reads=[R("ps", 2 + ti, 0, n), R("cst", 0, 0, 1)], writes=[R("ev", ti, 0, n)])
                    for (a0, a1, u0) in split(c0, c1):
                        m = a1 - a0
                        g.op(DVE, "scalar_tensor_tensor",
                             (U[:, u0:u0 + m], ps[ti][:, a0 - c0:a1 - c0], ccol(("b_pw1a", i), j),
                              e[:, a0 - c0:a1 - c0], ALU.add, ALU.mult),
                             reads=[R("ps", ti, a0 - c0, a1 - c0), R("ev", ti, a0 - c0, a1 - c0)],
                             writes=[R("UT", ub, u0, u0 + m)])
                if pa == 0:
                    g.op(DVE, "tensor_scalar", (U[:, 30:30 + HALO], U[:, 30:30 + HALO], ccol("halo_mask"), None, ALU.mult),
                         reads=[R("UT", ub, 30, 30 + HALO)], writes=[R("UT", ub, 30, 30 + HALO)])
                    g.op(ACT, "activation", (carry[:, i, j, :], U[:, W0:W0 + 30], AF.Copy),
                         reads=[R("UT", ub, W0, W0 + 30)], writes=[R("carry", i * KC + j, 0, 30)])
                else:
                    g.op(ACT, "activation", (STG[:, i, j, 0, :], U[:, 512:542], AF.Copy),
                         reads=[R("UT", ub, 512, 542)], writes=[R("STG", i, 0, 1)])
                    for s in range(2):
                        u0 = 542 + 46 * s + 16
                        g.op(ACT, "activation", (STG[:, i, j, 1 + s, :], U[:, u0:u0 + 30], AF.Copy),
                             reads=[R("UT", ub, u0, u0 + 30)], writes=[R("STG", i, 0, 1)])
                ct = ctmp[ub]
                g.op(DVE, "tensor_scalar",
                     (ct[:, 0:L], U[:, 0:L], cst[:, wdw + j:wdw + j + 1], ccol(("b_dw", i), j), ALU.mult, ALU.add),
                     reads=[R("UT", ub, 0, L), R("cst", 0, 0, 1)], writes=[R("ctmp", ub, 0, L)])
                for tap in range(1, 31):
                    wc = wdw + tap * 16 + j
                    g.op(DVE, "scalar_tensor_tensor",
                         (ct[:, 0:L], U[:, tap:tap + L], cst[:, wc:wc + 1], ct[:, 0:L], ALU.mult, ALU.add),
                         reads=[R("UT", ub, tap, tap + L), R("ctmp", ub, 0, L)], writes=[R("ctmp", ub, 0, L)])
                for (a0, a1, u0) in segs:
                    ci = u0 - 30
                    g.op(ACT, "activation", (b2[:, j, a0:a1], ct[:, ci:ci + (a1 - a0)], AF.Copy),
                         reads=[R("ctmp", ub, ci, ci + (a1 - a0))], writes=[R("b2", j, a0, a1)])
            for ti, (c0, c1) in enumerate(tiles):
                n = c1 - c0
                bm, bq = 4 + ti, 6 + ti
                for j in range(KC):
                    sq = sqt[j % 4]
                    g.op(ACT, "activation", (sq[:, 0:n], b2[:, j, c0:c1], AF.Square),
                         reads=[R("b2", j, c0, c1)], writes=[R("sqt", j % 4, 0, n)])
                    g.mm(ps[bm][:, 0:n], onesD[:], b2[:, j, c0:c1], j == 0, j == KC - 1,
                         reads=[R("b2", j, c0, c1)], writes=[R("ps", bm, 0, n)], inc=True)
                    g.mm(ps[bq][:, 0:n], onesD[:], sq[:, 0:n], j == 0, j == KC - 1,
                         reads=[R("sqt", j % 4, 0, n)], writes=[R("ps", bq, 0, n)], inc=True)
                mean, rstd, nmr = lnm[ti], lnr[ti], lnn[ti]
                g.op(ACT, "activation", (mean[:, 0:n], ps[bm][:, 0:n], AF.Copy),
                     reads=[R("ps", bm, 0, n)], writes=[R("ytmp", ti, 0, n)])
                g.op(DVE, "tensor_tensor", (nmr[:, 0:n], mean[:, 0:n], mean[:, 0:n], ALU.mult),
                     reads=[R("ytmp", ti, 0, n)], writes=[R("rs", ti, 0, n)])
                g.op(DVE, "tensor_tensor", (rstd[:, 0:n], ps[bq][:, 0:n], nmr[:, 0:n], ALU.subtract),
                     reads=[R("ps", bq, 0, n), R("rs", ti, 0, n)], writes=[R("fin", ti, 0, n)])
                g.op(ACT, "activation", (rstd[:, 0:n], rstd[:, 0:n], AF.Sqrt), {"bias": ccol("eps")},
                     reads=[R("fin", ti, 0, n)], writes=[R("fin", ti, 0, n)])
                g.op(DVE, "reciprocal", (rstd[:, 0:n], rstd[:, 0:n]),
                     reads=[R("fin", ti, 0, n)], writes=[R("fin", ti, 0, n)])
                g.op(DVE, "scalar_tensor_tensor", (nmr[:, 0:n], mean[:, 0:n], -1.0, rstd[:, 0:n], ALU.mult, ALU.mult),
                     reads=[R("ytmp", ti, 0, n), R("fin", ti, 0, n)], writes=[R("rs", ti, 0, n)])
                for j in range(KC):
                    e = ev[j % 4]
                    g.op(DVE, "tensor_tensor", (e[:, 0:n], b2[:, j, c0:c1], rstd[:, 0:n], ALU.mult),
                         reads=[R("b2", j, c0, c1), R("fin", ti, 0, n)], writes=[R("ev", j % 4, 0, n)])
                    g.op(DVE, "tensor_tensor", (e[:, 0:n], e[:, 0:n], nmr[:, 0:n], ALU.add),
                         reads=[R("ev", j % 4, 0, n), R("rs", ti, 0, n)], writes=[R("ev", j % 4, 0, n)])
                    g.op(ACT, "activation", (b2[:, j, c0:c1], e[:, 0:n], AF.Silu),
                         {"bias": ccol(("ln_b", i), j), "scale": ccol(("ln_g", i), j)},
                         reads=[R("ev", j % 4, 0, n), R("cst", 0, 0, 1)], writes=[R("b2", j, c0, c1)])
            for dch in range(KC):
                t = toggle()
                banks = (4 + 2 * t, 5 + 2 * t)
                proj(("pw2", i, dch), b2, "b2", KC, tiles, banks)
                for ti, (c0, c1) in enumerate(tiles):
                    n = c1 - c0
                    g.op(DVE, "scalar_tensor_tensor",
                         (xT[:, dch, c0:c1], ps[banks[ti]][:, 0:n], ccol(("b_pw2", i), dch), xT[:, dch, c0:c1],
                          ALU.add, ALU.add),
                         reads=[R("ps", banks[ti], 0, n), R("x", dch, c0, c1)], writes=[R("x", dch, c0, c1)])

        def head_norm_rope(bank, ti, c0, c1, gcolname):
            n = c1 - c0
            sq = sqt[ti]
            g.op(ACT, "activation", (sq[:, 0:n], ps[bank][:, 0:n], AF.Square),
                 reads=[R("ps", bank, 0, n)], writes=[R("sqt", ti, 0, n)])
            g.mm(ps[ti][:, 0:n], blk64[:], sq[:, 0:n], True, True,
                 reads=[R("sqt", ti, 0, n), R("blk64", 0, 0, 1)], writes=[R("ps", ti, 0, n)], inc=True)
            r = rs[ti]
            g.op(ACT, "activation", (r[:, 0:n], ps[ti][:, 0:n], AF.Sqrt), {"bias": ccol("eps")},
                 reads=[R("ps", ti, 0, n), R("cst", 0, 0, 1)], writes=[R("rs", ti, 0, n)])
            g.op(DVE, "reciprocal", (r[:, 0:n], r[:, 0:n]), reads=[R("rs", ti, 0, n)], writes=[R("rs", ti, 0, n)])
            y = ytmp[ti]
            g.op(DVE, "scalar_tensor_tensor", (y[:, 0:n], ps[bank][:, 0:n], ccol(gcolname), r[:, 0:n], ALU.mult, ALU.mult),
                 reads=[R("ps", bank, 0, n), R("rs", ti, 0, n)], writes=[R("ytmp", ti, 0, n)])
            yb = ybt[ti]
            g.op(ACT, "activation", (yb[:, 0:n], y[:, 0:n], AF.Copy),
                 reads=[R("ytmp", ti, 0, n)], writes=[R("ybt", ti, 0, n)])
            g.mm(ps[2 + ti][:, 0:n], perm[:], yb[:, 0:n], True, True,
                 reads=[R("ybt", ti, 0, n), R("perm", 0, 0, 1)], writes=[R("ps", 2 + ti, 0, n)], inc=True)
            f = fin[ti]
            g.op(DVE, "tensor_tensor", (y[:, 0:n], y[:, 0:n], rope[:, 0, c0:c1], ALU.mult),
                 reads=[R("ytmp", ti, 0, n), R("rope", 0, c0, c1)], writes=[R("ytmp", ti, 0, n)])
            g.op(DVE, "tensor_tensor", (f[:, 0:n], ps[2 + ti][:, 0:n], rope[:, 1, c0:c1], ALU.mult),
                 reads=[R("ps", 2 + ti, 0, n), R("rope", 0, c0, c1)], writes=[R("fin", ti, 0, n)])
            return f, y

        def kv_compute(pa, doK=True, doV=True):
            if pa == 0:
                ktiles = [(64, 384), (384, 704)]
                kbase = -64
                vt = [(64 + 128 * t, t) for t in range(5)]
            else:
                ktiles = [(0, 272), (272, 544)]
                kbase = 640
                vt = [(128 * t, 5 + t) for t in range(4)]
            ntiles = [(0, 352), (352, 704)] if pa == 0 else [(0, 272), (272, 544)]
            rmsnorm("kv_norm", ntiles)
            for ver in range(2 if doK else 0):
                for i in range(2):
                    t = toggle()
                    banks = (4 + 2 * t, 5 + 2 * t)
                    proj(("wk", ver, i), hT, "h", KC, ktiles, banks)
                    for ti, (c0, c1) in enumerate(ktiles):
                        n = c1 - c0
                        f, y = head_norm_rope(banks[ti], ti, c0, c1, "k_norm")
                        g.op(DVE, "tensor_tensor", (f[:, 0:n], f[:, 0:n], y[:, 0:n], ALU.add),
                             reads=[R("fin", ti, 0, n), R("ytmp", ti, 0, n)], writes=[R("fin", ti, 0, n)])
                        hl, hh = (2 * i, 2 * i + 1) if ver == 0 else (2 * i + 1, 2 * i)
                        pc1 = min(c1, 512) if pa == 1 else c1
                        if pc1 > c0:
                            m = pc1 - c0
                            k0 = c0 + kbase
                            g.op(ACT, "activation", (KT[0:64, hl, k0:k0 + m], f[0:64, 0:m], AF.Copy),
                                 reads=[R("fin", ti, 0, m)], writes=[R("KT", hl, k0, k0 + m)])
                            g.op(ACT, "activation", (KT[64:128, hh, k0:k0 + m], f[64:128, 0:m], AF.Copy),
                                 reads=[R("fin", ti, 0, m)], writes=[R("KT", hh, k0, k0 + m)])
                        if pa == 1 and c1 > 512:
                            o = 512 - c0
                            g.op(ACT, "activation",
                                 (KTs[0:64, hl, :, 128:144], f[0:64, o:o + 32].rearrange("p (s t) -> p s t", s=2), AF.Copy),
                                 reads=[R("fin", ti, o, o + 32)], writes=[R("KTs", 0, 128, 144)])
                            g.op(ACT, "activation",
                                 (KTs[64:128, hh, :, 128:144], f[64:128, o:o + 32].rearrange("p (s t) -> p s t", s=2), AF.Copy),
                                 reads=[R("fin", ti, o, o + 32)], writes=[R("KTs", 0, 128, 144)])
                            if ver == 0:
                                o2 = 384 - c0
                                g.op(ACT, "activation", (KOUT[:, i, :], f[:, o2:o2 + 160], AF.Copy),
                                     reads=[R("fin", ti, o2, o2 + 160)], writes=[R("KOUT", 0, 0, 1)])
            for half in range(2 if doV else 0):
                slot = get_unit(("wv", half))
                for (a, t) in vt:
                    tg = toggle()
                    bank = 4 + 2 * tg
                    for k in range(KC):
                        g.mm(ps[bank][:, 0:128], hT[:, k, a:a + 128], wr[slot][:, k, :], k == 0, k == KC - 1,
                             reads=[R("W", slot, 0, 1), R("h", k, a, a + 128)], writes=[R("ps", bank, 0, 128)],
                             inc=(k == KC - 1))
                    for gg in range(2):
                        gi = 2 * half + gg
                        g.op(ACT, "activation", (Vd[:, t, gi, 0:64], ps[bank][:, gg * 64:(gg + 1) * 64], AF.Copy),
                             reads=[R("ps", bank, 0, 128)], writes=[R("Vd", t * 4 + gi, 0, 1)])
                        g.op(DVE, "tensor_copy", (Vd[:, t, gi, 64:128], ps[bank][:, gg * 64:(gg + 1) * 64]),
                             reads=[R("ps", bank, 0, 128)], writes=[R("Vd", t * 4 + gi, 0, 1)])
                    if pa == 1 and t == 8:
                        g.op(DVE, "tensor_copy", (VOUT[:, half * 128:(half + 1) * 128], ps[bank][:, 0:128]),
                             reads=[R("ps", bank, 0, 128)], writes=[R("VOUT", 0, 0, 1)])
                if pa == 1:
                    for s in range(2):
                        tg = toggle()
                        bank = 4 + 2 * tg
                        a = 512 + 16 * s
                        for k in range(KC):
                            g.mm(ps[bank][0:16, 0:128], hT[:, k, a:a + 16], wr[slot][:, k, :], k == 0, k == KC - 1,
                                 reads=[R("W", slot, 0, 1), R("h", k, a, a + 16)], writes=[R("ps", bank, 0, 128)],
                                 inc=(k == KC - 1))
                        for gg in range(2):
                            gi = 2 * half + gg
                            g.op(ACT, "activation", (Vn[0:16, s, gi, 0:64], ps[bank][0:16, gg * 64:(gg + 1) * 64], AF.Copy),
                                 reads=[R("ps", bank, 0, 128)], writes=[R("Vn", 0, 0, 1)])
                            g.op(DVE, "tensor_copy", (Vn[0:16, s, gi, 64:128], ps[bank][0:16, gg * 64:(gg + 1) * 64]),
                                 reads=[R("ps", bank, 0, 128)], writes=[R("Vn", 0, 0, 1)])
                        g.op(DVE, "tensor_copy", (VSOUT[0:16, s, half * 128:(half + 1) * 128], ps[bank][0:16, 0:128]),
                             reads=[R("ps", bank, 0, 128)], writes=[R("VSOUT", 0, 0, 1)])

        def attn_block(jb, grp, q0, nq, ktl):
            t = toggle()
            ob, db = (4, 5) if t == 0 else (6, 7)
            N4 = 4 * nq
            N8 = 8 * nq
            pts = PT[t]
            for par in range(2):
                ph = slice(0, 64) if par == 0 else slice(64, 128)
                for ki, kd in enumerate(ktl):
                    M = kd["M"]
                    g.mm(ps[par * 2 + ki][0:M, 0:N4], kd["kt"](ph),
                         b2[ph, 4 * grp:4 * grp + 4, q0:q0 + nq], True, True, tp=((0, 0) if par == 0 else None),
                         reads=kd["kreads"] + [R("b2", 4 * grp + ii, q0, q0 + nq) for ii in range(4)],
                         writes=[R("ps", par * 2 + ki, 0, N4)], inc=True)
            for ki, kd in enumerate(ktl):
                r0, r1 = kd["r0"], kd["r1"]
                kw = {"scale": 0.125}
                bias_ap = kd["bias"] if kd["bias"] is not None else cst[:, _cc["zero"]:_cc["zero"] + 1]
                kw["bias"] = bias_ap[r0:r1, :]
                if r0 == 0 and r1 < 128:
                    g.op(DVE, "memset", (pts[ki][:, 0:N8], 0.0), writes=[R("PT", t * 2 + ki, 0, N8)])
                for par in range(2):
                    g.op(ACT, "activation", (pts[ki][r0:r1, par * N4:(par + 1) * N4], ps[par * 2 + ki][r0:r1, 0:N4], AF.Exp), kw,
                         reads=[R("ps", par * 2 + ki, 0, N4), R("cst", 0, 0, 1)],
                         writes=[R("PT", t * 2 + ki, par * N4, (par + 1) * N4)])
            nk = len(ktl)
            for ki, kd in enumerate(ktl):
                r0, r1 = kd["r0"], kd["r1"]
                tpv = None
                vap = kd["v"]
                if r0 == 0 and r1 < 128:
                    r1 = 128
                    vap = kd["vfull"]
                g.mm(ps[ob][:, 0:N8], vap, pts[ki][r0:r1, 0:N8], ki == 0, ki == nk - 1, tp=tpv,
                     reads=kd["vreads"] + [R("PT", t * 2 + ki, 0, N8)], writes=[R("ps", ob, 0, N8)], inc=False)
                g.mm(ps[db][:, 0:N8], ones1[r0:r1, :], pts[ki][r0:r1, 0:N8], ki == 0, ki == nk - 1, tp=tpv,
                     reads=[R("PT", t * 2 + ki, 0, N8), R("ones1", 0, 0, 1)], writes=[R("ps", db, 0, N8)],
                     inc=(ki == nk - 1))
            dn = den[t]
            sbase = jb * 32 + grp * 8
            g.op(DVE, "tensor_copy", (dn[:, 0:N8], ps[db][:, 0:N8]),
                 reads=[R("ps", db, 0, N8)], writes=[R("den", t, 0, N8)])
            g.op(DVE, "tensor_tensor",
                 (dn[:, 0:N8].rearrange("p (a b) -> p a b", a=8), dn[:, 0:N8].rearrange("p (a b) -> p a b", a=8),
                  SE[:, sbase:sbase + 8].unsqueeze(2).to_broadcast([128, 8, nq]), ALU.add),
                 reads=[R("den", t, 0, N8), R("SE", 0, 0, 1)], writes=[R("den", t, 0, N8)])
            g.op(DVE, "reciprocal", (dn[:, 0:N8], dn[:, 0:N8]), reads=[R("den", t, 0, N8)], writes=[R("den", t, 0, N8)])
            for par in range(2):
                ph = slice(0, 64) if par == 0 else slice(64, 128)
                for ii in range(4):
                    cs = par * N4 + ii * nq
                    g.op(DVE, "tensor_tensor",
                         (hT[ph, 4 * grp + ii, q0:q0 + nq], ps[ob][ph, cs:cs + nq], dn[ph, cs:cs + nq], ALU.mult),
                         reads=[R("ps", ob, 0, N8), R("den", t, 0, N8)],
                         writes=[R("h", 4 * grp + ii, q0, q0 + nq)])

        def attn_layer(jb, pa, tiles):
            l = 2 + jb
            rmsnorm(("attn_norm", jb), tiles)
            for j in range(KC):
                t = toggle()
                banks = (4 + 2 * t, 5 + 2 * t)
                proj(("wq", jb, j), hT, "h", KC, tiles, banks)
                for ti, (c0, c1) in enumerate(tiles):
                    n = c1 - c0
                    f, y = head_norm_rope(banks[ti], ti, c0, c1, ("q_norm", jb))
                    g.op(DVE, "tensor_tensor", (b2[:, j, c0:c1], f[:, 0:n], y[:, 0:n], ALU.add),
                         reads=[R("fin", ti, 0, n), R("ytmp", ti, 0, n)], writes=[R("b2", j, c0, c1)])
            hb = cst[:, _cc["halo_bias"]:_cc["halo_bias"] + 1]
            for cc in range(8):
                cg = cc + 8 * pa
                q0 = (HALO if pa == 0 else 0) + 64 * cc
                t0 = cg // 2
                for grp in range(4):
                    def mk(tile, r0, r1, bias, grp=grp):
                        return dict(kt=(lambda ph, tile=tile, grp=grp: KT[ph, grp, 128 * tile:128 * tile + 128]), M=128,
                                    r0=r0, r1=r1, v=Vd[r0:r1, tile, grp, :], vfull=Vd[:, tile, grp, :], bias=bias,
                                    kreads=[R("KT", grp, 128 * tile, 128 * tile + 128)],
                                    vreads=[R("Vd", tile * 4 + grp, 0, 1)])
                    if cg % 2 == 0:
                        ktl = [mk(t0, 0, 128, hb if t0 == 0 else None), mk(t0 + 1, 0, 64, None)]
                    else:
                        ktl = [mk(t0, 64, 128, hb if t0 == 0 else None), mk(t0 + 1, 0, 128, None)]
                    attn_block(jb, grp, q0, 64, ktl)
            if pa == 1:
                for s in range(2):
                    q0 = 512 + 16 * s
                    for grp in range(4):
                        ktl = [dict(kt=(lambda ph, s=s, grp=grp: KTs[ph, grp, s, 0:128]), M=128, r0=0, r1=128,
                                    v=Vds[:, s, grp, :], vfull=Vds[:, s, grp, :], bias=None, kreads=[R("KTs", 0, 0, 128)], vreads=[R("Vds", 0, 0, 1)]),
                               dict(kt=(lambda ph, s=s, grp=grp: KTs[ph, grp, s, 128:144]), M=16, r0=0, r1=16,
                                    v=Vn[0:16, s, grp, :], vfull=Vn[:, s, grp, :], bias=None, kreads=[R("KTs", 0, 128, 144)], vreads=[R("Vn", 0, 0, 1)])]
                        attn_block(jb, grp, q0, 16, ktl)
            for dch in range(KC):
                t = toggle()
                banks = (4 + 2 * t, 5 + 2 * t)
                proj(("wo", jb, dch), hT, "h", KC, tiles, banks)
                for ti, (c0, c1) in enumerate(tiles):
                    n = c1 - c0
                    g.op(DVE, "tensor_tensor", (xT[:, dch, c0:c1], ps[banks[ti]][:, 0:n], xT[:, dch, c0:c1], ALU.add),
                         reads=[R("ps", banks[ti], 0, n), R("x", dch, c0, c1)], writes=[R("x", dch, c0, c1)])
            if "noffn" not in stages:
                ffn(l, tiles)

        for pa in passes:
            W = W0 if pa == 0 else W1
            g.dma(SP, xT[:, :, 0:W], xin[pa], s_x, writes=[R("x", k, 0, W) for k in range(KC)])
            g.dma(SP, rope[:, :, 0:W], ropein[pa], s_rope, writes=[R("rope", 0, 0, W)])
            tA = [(0, 352), (352, 704)] if pa == 0 else [(0, 272), (272, 544)]
            tB = [(192, 448), (448, 704)] if pa == 0 else [(0, 272), (272, 544)]
            for i in range(2):
                if "conv%d" % i in stages:
                    conv_layer(i, pa, tA)
                if "ffn%d" % i in stages:
                    ffn(i, tA)
            if "kv" in stages or "kvk" in stages or "kvv" in stages:
                kv_compute(pa, "kv" in stages or "kvk" in stages, "kv" in stages or "kvv" in stages)
            for jb in range(2):
                if "attn%d" % jb in stages:
                    attn_layer(jb, pa, tB)
            if "dbg" in stages and pa == 0:
                sd = g.new_sem("dbgs")
                tl_ = g.tog
                ca, cb = ctmp[0], ctmp[1]
                g.op(ACT, "activation", (ca[:, 0:256].rearrange("p (a b) -> p a b", a=4), b2[:, 12:16, 640:704], AF.Copy),
                     reads=[R("b2", kk, 640, 704) for kk in range(12, 16)], writes=[R("ctmp", 0, 0, 704)])
                g.op(ACT, "activation", (ca[:, 256:512].rearrange("p (a b) -> p a b", a=4), hT[:, 12:16, 640:704], AF.Copy),
                     reads=[R("h", kk, 640, 704) for kk in range(12, 16)], writes=[R("ctmp", 0, 0, 704)])
                g.op(ACT, "activation", (ca[:, 512:704], KT[:, 3, 448:640], AF.Copy),
                     reads=[R("KT", 3, 0, 1152)], writes=[R("ctmp", 0, 0, 704)])
                g.op(ACT, "activation", (cb[:, 0:352], PT[tl_][0][:, 0:352], AF.Copy),
                     reads=[R("PT", tl_ * 2, 0, 512)], writes=[R("ctmp", 1, 0, 704)])
                g.op(ACT, "activation", (cb[:, 352:704], PT[tl_][1][:, 0:352], AF.Copy),
                     reads=[R("PT", tl_ * 2 + 1, 0, 512)], writes=[R("ctmp", 1, 0, 704)])
                g.dma(SP, dbg_out["o_dbg_a"], ca[:], sd, reads=[R("ctmp", 0, 0, 704)])
                g.dma(SP, dbg_out["o_dbg_b"], cb[:], sd, reads=[R("ctmp", 1, 0, 704)])
                g.dma(SP, dbg_out["o_dbg_den0"], den[0][:], sd, reads=[R("den", 0, 0, 512)])
                g.dma(SP, dbg_out["o_dbg_den1"], den[1][:], sd, reads=[R("den", 1, 0, 512)])
                g.out_sems.append(sd)
                print("DBG last toggle", tl_)
            sy = g.new_sem("y%d" % pa)
            if pa == 0:
                g.dma(SP, o_y[0], xT[:, :, HALO:W0], sy, reads=[R("x", k, HALO, W0) for k in range(KC)])
            else:
                g.dma(SP, o_y[1], xT[:, :, 0:W1], sy, reads=[R("x", k, 0, W1) for k in range(KC)])
            g.out_sems.append(sy)
        assert g.ui == len(passes) * NU
        so = g.new_sem("outs")
        so2 = g.new_sem("outs2")
        g.dma(POOL, o_st, STG[:], so2, reads=[R("STG", 0, 0, 1), R("STG", 1, 0, 1)])
        g.out_sems.append(so2)
        g.dma(SP, o_k, KOUT[:], so, reads=[R("KOUT", 0, 0, 1)])
        g.dma(SP, o_v, VOUT[:], so, reads=[R("VOUT", 0, 0, 1)])
        g.dma(SP, o_vs, VSOUT[:], so, reads=[R("VSOUT", 0, 0, 1)])
        g.out_sems.append(so)
        for s in g.out_sems:
            SP.ops.append(("wait", s, s.count))

        for s in g.sems:
            s.h = es.enter_context(nc.semaphore(s.name))
        block = es.enter_context(nc.Block())

        def emitter(st):
            def run(e):
                for o in st.ops:
                    if o[0] == "wait":
                        e.wait_ge(o[1].h, o[2])
                    else:
                        _, method, args, kw, sem, inc = o
                        ins = getattr(e, method)(*args, **kw)
                        if sem is not None:
                            ins.then_inc(sem.h, inc)
            return run

        block.tensor(emitter(PE))
        block.scalar(emitter(ACT))
        block.vector(emitter(DVE))
        block.gpsimd(emitter(POOL))
        block.sync(emitter(SP))
    g.stats = {k: len(getattr(g, k).ops) for k in ("PE", "ACT", "DVE", "POOL", "SP")}
    _check_deadlock([g.PE, g.ACT, g.DVE, g.POOL, g.SP])
    return nc, g


def _fm(a):
    T = a.shape[0]
    return np.ascontiguousarray(a.T.reshape(KC, 128, T).transpose(1, 0, 2))


def _unfm(a):
    p, k, T = a.shape
    return np.ascontiguousarray(a.transpose(2, 1, 0).reshape(T, k * p))


def _colvec(v):
    return np.ascontiguousarray(np.asarray(v, np.float32).reshape(KC, 128).T)


def _pack_wall(inp):
    wall = np.zeros((NU, 128, KC, 128), np.float32)

    def pk(Wm, r0, nk, cols):
        blk = Wm[r0:r0 + nk * 128][:, cols]
        return blk.reshape(nk, 128, 128).transpose(1, 0, 2)

    for u, d in enumerate(SEQ):
        kind = d[0]
        if kind == "pw1":
            wall[u] = pk(inp["w_pw1"][d[1]], 0, KC, slice(d[2] * 128, d[2] * 128 + 128))
        elif kind == "pw2":
            wall[u] = pk(inp["w_pw2"][d[1]], 0, KC, slice(d[2] * 128, d[2] * 128 + 128))
        elif kind == "wg":
            wall[u] = pk(inp["w_gate"][d[1]], 0, KC, slice(d[2] * 128, d[2] * 128 + 128))
        elif kind == "wu":
            wall[u] = pk(inp["w_up"][d[1]], 0, KC, slice(d[2] * 128, d[2] * 128 + 128))
        elif kind == "wd":
            part = PARTS[d[2]]
            wall[u, :, 0:len(part), :] = pk(inp["w_down"][d[1]], part[0] * 128, len(part), slice(d[3] * 128, d[3] * 128 + 128))
        elif kind == "wk":
            ver, i = d[1], d[2]
            if ver == 0:
                cols = np.arange(i * 128, i * 128 + 128)
            else:
                cols = np.concatenate([np.arange((2 * i + 1) * 64, (2 * i + 2) * 64), np.arange(2 * i * 64, (2 * i + 1) * 64)])
            wall[u] = pk(inp["w_k"], 0, KC, cols)
        elif kind == "wv":
            wall[u] = pk(inp["w_v"], 0, KC, slice(d[1] * 128, d[1] * 128 + 128))
        elif kind == "wq":
            wall[u] = pk(inp["w_q"][d[1]], 0, KC, slice(d[2] * 128, d[2] * 128 + 128))
        elif kind == "wo":
            wall[u] = pk(inp["w_o"][d[1]], 0, KC, slice(d[2] * 128, d[2] * 128 + 128))
        else:
            raise ValueError(d)
    return wall


def _rope_table(pos):
    half = 32
    inv_freq = (10000.0 ** (-np.arange(half, dtype=np.float32) / half)).astype(np.float32)
    ang = pos.astype(np.float32)[:, None] * inv_freq[None, :]
    cos = np.cos(ang).astype(np.float32)
    sin = np.sin(ang).astype(np.float32)
    tab = np.zeros((128, 2, len(pos)), np.float32)
    for p in range(128):
        d = p % 64
        tab[p, 0] = cos[:, d % 32]
        tab[p, 1] = -sin[:, d % 32] if d < 32 else sin[:, d % 32]
    return tab


_CACHE = {}


def kernel(_plan=None, **inp):
    inp = {k: np.asarray(v) for k, v in inp.items()}
    if _plan is not None:
        nc, g = build_program(*_plan)
    else:
        if "nc" not in _CACHE:
            _CACHE["nc"] = build_program()
        nc, g = _CACHE["nc"]

    wall = _pack_wall(inp)
    xp = inp["x_prompt"][0]
    xs = inp["x_sample"]
    perm = np.zeros((128, 128), np.float32)
    for m in range(128):
        d = m % 64
        perm[(m - d) + ((d + 32) % 64), m] = 1.0

    cbase = np.zeros((128, NCONST), np.float32)
    for i in range(2):
        cbase[:, _cc[("conv_norm", i)]:][:, :16] = _colvec(inp["conv_norm"][i])
        cbase[:, _cc[("b_pw1a", i)]:][:, :16] = _colvec(inp["b_pw1"][i][:D])
        cbase[:, _cc[("b_pw1g", i)]:][:, :16] = _colvec(inp["b_pw1"][i][D:])
        for tap in range(31):
            cbase[:, _cc[("w_dw", i)] + tap * 16:][:, :16] = _colvec(inp["w_dw"][i][tap])
        cbase[:, _cc[("b_dw", i)]:][:, :16] = _colvec(inp["b_dw"][i])
        cbase[:, _cc[("ln_g", i)]:][:, :16] = _colvec(inp["conv_ln_g"][i])
        cbase[:, _cc[("ln_b", i)]:][:, :16] = _colvec(inp["conv_ln_b"][i])
        cbase[:, _cc[("b_pw2", i)]:][:, :16] = _colvec(inp["b_pw2"][i])
    for l in range(4):
        cbase[:, _cc[("ffn_norm", l)]:][:, :16] = _colvec(inp["ffn_norm"][l])
    cbase[:, _cc["kv_norm"]:][:, :16] = _colvec(inp["kv_norm"])
    for j in range(2):
        cbase[:, _cc[("attn_norm", j)]:][:, :16] = _colvec(inp["attn_norm"][j])
        cbase[:, _cc[("q_norm", j)]] = np.tile(inp["q_norm"][j], 2)
    cbase[:, _cc["k_norm"]] = np.tile(inp["k_norm"], 2)
    cbase[:, _cc["eps"]] = EPS

    sinks = np.zeros((128, 64), np.float32)
    for jb in range(2):
        for grp in range(4):
            for par in range(2):
                for i in range(4):
                    sinks[:, jb * 32 + grp * 8 + par * 4 + i] = inp["sinks"][jb, 8 * grp + 2 * i + par]

    in_maps = []
    for c in range(NCORE):
        base = 1024 * c
        x0 = np.zeros((W0, D), np.float32)
        lo = base - HALO
        if lo < 0:
            x0[-lo:] = xp[0:base + OWN]
        else:
            x0[:] = xp[lo:base + OWN]
        x1 = np.concatenate([xp[base + OWN:base + 1024], xs[2 * c], xs[2 * c + 1]], axis=0)
        pos0 = np.arange(lo, base + OWN)
        pos1 = np.concatenate([np.arange(base + OWN, base + 1024), PAST_LEN + np.arange(16), PAST_LEN + np.arange(16)])
        cc = cbase.copy()
        cc[:, _cc["halo_mask"]] = 0.0 if c == 0 else 1.0
        cc[:, _cc["halo_bias"]] = -30000.0 if c == 0 else 0.0
        scv = np.zeros((128, 2, KC, 2, 30), np.float32)
        for i in range(2):
            for s in range(2):
                scv[:, i, :, s, :] = _fm(inp["state_conv"][i, 2 * c + s])
        ckT = np.zeros((128, 4, 2, 128), np.float32)
        cvd = np.zeros((128, 2, 4, 128), np.float32)
        for s in range(2):
            ck = inp["cache_k"][2 * c + s]
            cv = inp["cache_v"][2 * c + s]
            kt = ck.transpose(2, 1, 0)
            ckT[0:64, :, s, :] = kt
            ckT[64:128, :, s, :] = kt
            cvd[:, s, :, 0:64] = cv
            cvd[:, s, :, 64:128] = cv
        in_maps.append({
            "xin0": _fm(x0), "xin1": _fm(x1),
            "rope0": _rope_table(pos0), "rope1": _rope_table(pos1),
            "consts": cc, "sinks": sinks, "perm": perm, "scv": scv, "ckT": ckT, "cvd": cvd,
            "ck_raw": np.ascontiguousarray(inp["cache_k"][2 * c:2 * c + 2].reshape(2, 128, 256)),
            "cv_raw": np.ascontiguousarray(inp["cache_v"][2 * c:2 * c + 2].reshape(2, 128, 256)),
            "wall": wall,
        })

    res = run_bass_kernel_spmd(nc, in_maps, core_ids=list(range(NCORE)))
    outs = res.results
    if _plan is not None:
        return outs

    y_prompt = np.zeros((1, 8192, D), np.float32)
    y_sample = np.zeros((16, 16, D), np.float32)
    st_p = np.zeros((2, 1, 30, D), np.float32)
    st_s = np.zeros((2, 16, 30, D), np.float32)
    k_p = np.zeros((1, 128, 4, 64), np.float32)
    v_p = np.zeros((1, 128, 4, 64), np.float32)
    k_s = np.zeros((16, 128, 4, 64), np.float32)
    v_s = np.zeros((16, 128, 4, 64), np.float32)
    for c in range(NCORE):
        o = outs[c]
        base = 1024 * c
        y_prompt[0, base:base + OWN] = _unfm(o["o_y0"])
        y1 = o["o_y1"]
        y_prompt[0, base + OWN:base + 1024] = _unfm(y1[:, :, 0:OWN])
        for s in range(2):
            b = 2 * c + s
            y_sample[b] = _unfm(y1[:, :, OWN + 16 * s:OWN + 16 * s + 16])
            for i in range(2):
                st_s[i, b] = _unfm(o["o_st"][:, i, :, 1 + s, :])
            knew = _unfm(o["o_k"][:, :, 128 + 16 * s:128 + 16 * s + 16])
            k_s[b] = np.concatenate([o["o_kold"][s], knew], axis=0).reshape(128, 4, 64)
            v_s[b] = np.concatenate([o["o_vold"][s], o["o_vs"][:, s, :]], axis=0).reshape(128, 4, 64)
        if c == NCORE - 1:
            for i in range(2):
                st_p[i, 0] = _unfm(o["o_st"][:, i, :, 0, :])
            k_p[0] = _unfm(o["o_k"][:, :, 0:128]).reshape(128, 4, 64)
            v_p[0] = o["o_v"].reshape(128, 4, 64)
    return (y_prompt, y_sample, st_p, k_p, v_p, st_s, k_s, v_s)
```

```python
import contextlib
import numpy as np
import concourse.bass as bass
import concourse.mybir as mybir
from concourse.bass_utils import run_bass_kernel_spmd

F32 = mybir.dt.float32
BF16 = mybir.dt.bfloat16
ALU = mybir.AluOpType
AF = mybir.ActivationFunctionType

D = 2048
KC = 16
FC = 44
NCORE = 8
HALO = 192
OWN = 512
W0 = HALO + OWN
W1 = OWN + 32
WMAX = W0
NS = 5
PARTS = [list(range(0, 15)), list(range(15, 30)), list(range(30, 44))]
EPS = 1e-6
PAST_LEN = 2048
SAME_ENGINE_SYNC = False

_cc = {}
_ncol = 0


def _alloc_cols(name, n):
    global _ncol
    _cc[name] = _ncol
    _ncol += n


for _i in range(2):
    _alloc_cols(("conv_norm", _i), 16)
    _alloc_cols(("b_pw1a", _i), 16)
    _alloc_cols(("b_pw1g", _i), 16)
    _alloc_cols(("w_dw", _i), 31 * 16)
    _alloc_cols(("b_dw", _i), 16)
    _alloc_cols(("ln_g", _i), 16)
    _alloc_cols(("ln_b", _i), 16)
    _alloc_cols(("b_pw2", _i), 16)
for _l in range(4):
    _alloc_cols(("ffn_norm", _l), 16)
_alloc_cols("kv_norm", 16)
for _j in range(2):
    _alloc_cols(("attn_norm", _j), 16)
_alloc_cols("k_norm", 1)
_alloc_cols(("q_norm", 0), 1)
_alloc_cols(("q_norm", 1), 1)
_alloc_cols("halo_mask", 1)
_alloc_cols("halo_bias", 1)
_alloc_cols("eps", 1)
_alloc_cols("zero", 1)
_alloc_cols("pad", 9)
NCONST = _ncol


ALL_STAGES = ("conv0", "ffn0", "conv1", "ffn1", "kv", "attn0", "attn1")


def unit_seq(stages=ALL_STAGES):
    seq = []

    def ffn_units(l):
        for p, part in enumerate(PARTS):
            for f in part:
                seq.append(("wg", l, f))
                seq.append(("wu", l, f))
            for dch in range(16):
                seq.append(("wd", l, p, dch))

    for i in range(2):
        if "conv%d" % i in stages:
            for j in range(16):
                seq.append(("pw1", i, j))
                seq.append(("pw1", i, 16 + j))
            for dch in range(16):
                seq.append(("pw2", i, dch))
        if "ffn%d" % i in stages:
            ffn_units(i)
        if i == 1 and ("kv" in stages or "kvk" in stages):
            seq += [("wk", 0, 0), ("wk", 0, 1), ("wk", 1, 0), ("wk", 1, 1)]
        if i == 1 and ("kv" in stages or "kvv" in stages):
            seq += [("wv", 0), ("wv", 1)]
    for jb in range(2):
        if "attn%d" % jb in stages:
            for j in range(16):
                seq.append(("wq", jb, j))
            for dch in range(16):
                seq.append(("wo", jb, dch))
            if "noffn" not in stages:
                ffn_units(2 + jb)
    return seq


SEQ = unit_seq()
NU = len(SEQ)
_PLAN = {"stages": ALL_STAGES, "passes": (0, 1)}


class Sem:
    def __init__(self, name):
        self.name = name
        self.h = None
        self.count = 0


class Stream:
    def __init__(self, name, is_pe=False):
        self.name = name
        self.sem = Sem("s_" + name)
        self.ops = []
        self.known = {}
        self.is_pe = is_pe

    def need(self, sem, val):
        if sem is self.sem and (self.is_pe or not SAME_ENGINE_SYNC):
            return
        if self.known.get(sem, 0) >= val:
            return
        self.known[sem] = val
        self.ops.append(("wait", sem, val))


class Tracker:
    def __init__(self):
        self.w = {}
        self.r = {}

    def deps(self, reads, writes):
        out = []
        for (b, i, c0, c1) in reads:
            for (a0, a1, dep) in self.w.get((b, i), ()):
                if a0 < c1 and c0 < a1:
                    out.append(dep)
        for (b, i, c0, c1) in writes:
            for (a0, a1, dep) in self.w.get((b, i), ()):
                if a0 < c1 and c0 < a1:
                    out.append(dep)
            for (a0, a1, dep) in self.r.get((b, i), ()):
                if a0 < c1 and c0 < a1:
                    out.append(dep)
        return out

    def commit(self, reads, writes, dep):
        for (b, i, c0, c1) in writes:
            key = (b, i)
            self.w[key] = [x for x in self.w.get(key, ()) if not (c0 <= x[0] and x[1] <= c1)] + [(c0, c1, dep)]
            if key in self.r:
                self.r[key] = [x for x in self.r[key] if not (c0 <= x[0] and x[1] <= c1)]
        for (b, i, c0, c1) in reads:
            key = (b, i)
            lst = self.r.get(key, [])
            lst = [x for x in lst if not (x[2][0] is dep[0] and c0 <= x[0] and x[1] <= c1)]
            lst.append((c0, c1, dep))
            self.r[key] = lst


class Gen:
    def __init__(self, nc):
        self.nc = nc
        self.tr = Tracker()
        self.PE = Stream("pe", is_pe=True)
        self.ACT = Stream("act")
        self.DVE = Stream("dve")
        self.POOL = Stream("pool")
        self.SP = Stream("sp")
        self.sems = [self.PE.sem, self.ACT.sem, self.DVE.sem, self.POOL.sem, self.SP.sem]
        self.ui = 0
        self.tog = 0
        self.out_sems = []

    def new_sem(self, name):
        s = Sem(name)
        self.sems.append(s)
        return s

    def op(self, st, method, args, kw=None, reads=(), writes=(), inc=True):
        kw = kw or {}
        for (s, v) in self.tr.deps(reads, writes):
            st.need(s, v)
        if inc:
            st.sem.count += 1
            dep = (st.sem, st.sem.count)
            st.ops.append(("op", method, args, kw, st.sem, 1))
        else:
            dep = (st.sem, st.sem.count + 1)
            st.ops.append(("op", method, args, kw, None, 0))
        self.tr.commit(reads, writes, dep)

    def dma(self, st, out, in_, sem, reads=(), writes=()):
        for (s, v) in self.tr.deps(reads, writes):
            st.need(s, v)
        sem.count += 16
        st.ops.append(("op", "dma_start", (), {"out": out, "in_": in_}, sem, 16))
        self.tr.commit(reads, writes, (sem, sem.count))

    def mm(self, out, lhsT, rhs, start, stop, reads, writes, inc, tp=None):
        kw = {"start": start, "stop": stop}
        if tp is not None:
            kw["tile_position"] = tp
        self.op(self.PE, "matmul", (out, lhsT, rhs), kw, reads, writes, inc)


def _check_deadlock(streams):
    val = {}
    pc = [0] * len(streams)
    progress = True
    while progress:
        progress = False
        for si, st in enumerate(streams):
            ops = st.ops
            i = pc[si]
            while i < len(ops):
                o = ops[i]
                if o[0] == "wait":
                    if val.get(o[1], 0) < o[2]:
                        break
                else:
                    if o[4] is not None:
                        val[o[4]] = val.get(o[4], 0) + o[5]
                i += 1
            if i != pc[si]:
                progress = True
                pc[si] = i
    stuck = [(st.name, pc[si], st.ops[pc[si]][1].name, st.ops[pc[si]][2], val.get(st.ops[pc[si]][1], 0))
             for si, st in enumerate(streams) if pc[si] < len(st.ops)]
    if stuck:
        raise RuntimeError("semaphore deadlock: %r" % (stuck,))


def R(buf, idx, c0, c1):
    return (buf, idx, c0, c1)


def build_program(stages=ALL_STAGES, passes=(0, 1)):
    global SEQ, NU
    SEQ = unit_seq(stages)
    NU = len(SEQ)
    nc = bass.Bass("TRN2", target_bir_lowering=False)
    g = Gen(nc)

    def din(name, shape):
        return nc.dram_tensor(name, list(shape), F32, kind="ExternalInput").ap()

    def dout(name, shape):
        return nc.dram_tensor(name, list(shape), F32, kind="ExternalOutput").ap()

    xin = [din("xin0", [128, KC, W0]), din("xin1", [128, KC, W1])]
    ropein = [din("rope0", [128, 2, W0]), din("rope1", [128, 2, W1])]
    cin = din("consts", [128, NCONST])
    sinkin = din("sinks", [128, 64])
    permin = din("perm", [128, 128])
    scvin = din("scv", [128, 2, KC, 2, 30])
    cktin = din("ckT", [128, 4, 2, 128])
    cvdin = din("cvd", [128, 2, 4, 128])
    ckraw = din("ck_raw", [2, 128, 256])
    cvraw = din("cv_raw", [2, 128, 256])
    wall = din("wall", [NU, 128, KC, 128])

    o_y = [dout("o_y0", [128, KC, OWN]), dout("o_y1", [128, KC, W1])]
    o_st = dout("o_st", [128, 2, KC, 3, 30])
    o_k = dout("o_k", [128, 2, 160])
    o_v = dout("o_v", [128, 256])
    o_vs = dout("o_vs", [16, 2, 256])
    o_kold = dout("o_kold", [2, 112, 256])
    o_vold = dout("o_vold", [2, 112, 256])

    dbg_out = {}
    if "dbg" in stages:
        for nm, shp in (("o_dbg_a", [128, 704]), ("o_dbg_b", [128, 704]), ("o_dbg_den0", [128, 512]), ("o_dbg_den1", [128, 512])):
            dbg_out[nm] = nc.dram_tensor(nm, shp, F32, kind="ExternalOutput").ap()

    with contextlib.ExitStack() as es:
        def sb(name, shape, dt):
            return es.enter_context(nc.sbuf_tensor(name, list(shape), dt))

        xT = sb("xT", [128, KC, WMAX], F32)
        hT = sb("hT", [128, KC, WMAX], BF16)
        b2 = sb("b2", [128, KC, WMAX], BF16)
        wr = [sb("wr%d" % i, [128, KC, 128], BF16) for i in range(NS)]
        KT = sb("KT", [128, 4, 1152], BF16)
        KTs = sb("KTs", [128, 4, 2, 144], BF16)
        Vd = sb("Vd", [128, 9, 4, 128], BF16)
        Vds = sb("Vds", [128, 2, 4, 128], BF16)
        Vn = sb("Vn", [128, 2, 4, 128], BF16)
        rope = sb("rope", [128, 2, WMAX], F32)
        cst = sb("cst", [128, NCONST], F32)
        snk = sb("snk", [128, 64], F32)
        SE = sb("SE", [128, 64], F32)
        permf = sb("permf", [128, 128], F32)
        perm = sb("perm_b", [128, 128], BF16)
        onesD = sb("onesD", [128, 128], BF16)
        blk64 = sb("blk64", [128, 128], BF16)
        ones1 = sb("ones1", [128, 128], BF16)
        scv = sb("scv_b", [128, 2, KC, 2, 30], BF16)
        carry = sb("carry", [128, 2, KC, 30], BF16)
        UT = [sb("UT%d" % i, [128, 734], BF16) for i in range(2)]
        ctmp = [sb("ctmp%d" % i, [128, 704], F32) for i in range(2)]
        sqt = [sb("sqt%d" % i, [128, 352], BF16) for i in range(4)]
        rs = [sb("rs%d" % i, [128, 352], F32) for i in range(2)]
        ev = [sb("ev%d" % i, [128, 352], F32) for i in range(4)]
        ytmp = [sb("ytmp%d" % i, [128, 352], F32) for i in range(2)]
        ybt = [sb("ybt%d" % i, [128, 352], BF16) for i in range(2)]
        fin = [sb("fin%d" % i, [128, 352], F32) for i in range(2)]
        lnm, lnr, lnn = ytmp, fin, rs
        PT = [[sb("PT%d_%d" % (i, j), [128, 512], BF16) for j in range(2)] for i in range(2)]
        den = [sb("den%d" % i, [128, 512], F32) for i in range(2)]
        STG = sb("STG", [128, 2, KC, 3, 30], BF16)
        KOUT = sb("KOUT", [128, 2, 160], F32)
        VOUT = sb("VOUT", [128, 256], F32)
        VSOUT = sb("VSOUT", [16, 2, 256], F32)
        ps = [es.enter_context(nc.psum_tensor("ps%d" % i, [128, 512], F32)) for i in range(8)]

        PE, ACT, DVE, POOL, SP = g.PE, g.ACT, g.DVE, g.POOL, g.SP
        wsem = [g.new_sem("w%d" % i) for i in range(NS)]
        s_cst = g.new_sem("cst")
        s_x = g.new_sem("x")
        s_rope = g.new_sem("rope")
        s_kv = g.new_sem("kvin")

        def ccol(name, k=0):
            c = _cc[name] + k
            return cst[:, c:c + 1]

        g.dma(SP, cst[:], cin, s_cst, writes=[R("cst", 0, 0, 1)])
        g.dma(SP, snk[:], sinkin, s_cst, writes=[R("snk", 0, 0, 1)])
        g.dma(SP, permf[:], permin, s_cst, writes=[R("permf", 0, 0, 1)])
        tot = (s_cst, s_cst.count)
        for key in (("cst", 0), ("snk", 0), ("permf", 0)):
            g.tr.w[key] = [(0, 1, tot)]
        g.dma(POOL, scv[:], scvin, s_kv, writes=[R("scv", 0, 0, 1)])
        g.dma(POOL, KTs[:, :, :, 0:128], cktin, s_kv, writes=[R("KTs", 0, 0, 128)])
        g.dma(POOL, Vds[:], cvdin, s_kv, writes=[R("Vds", 0, 0, 1)])
        tot = (s_kv, s_kv.count)
        g.tr.w[("scv", 0)] = [(0, 1, tot)]
        g.tr.w[("KTs", 0)] = [(0, 128, tot)]
        g.tr.w[("Vds", 0)] = [(0, 1, tot)]

        g.op(DVE, "memset", (Vn[:].rearrange("p a b c -> p (a b c)"), 0.0), writes=[R("Vn", 0, 0, 1)])
        g.op(DVE, "memset", (onesD[:], 1.0 / D), writes=[R("onesD", 0, 0, 1)])
        g.op(DVE, "memset", (ones1[:], 1.0), writes=[R("ones1", 0, 0, 1)])
        g.op(DVE, "memset", (blk64[:], 0.0), writes=[R("blk64", 0, 0, 1)])
        g.op(DVE, "memset", (blk64[0:64, 0:64], 1.0 / 64), writes=[R("blk64", 0, 0, 1)])
        g.op(DVE, "memset", (blk64[64:128, 64:128], 1.0 / 64), writes=[R("blk64", 0, 0, 1)])
        g.op(DVE, "tensor_copy", (perm[:], permf[:]), reads=[R("permf", 0, 0, 1)], writes=[R("perm", 0, 0, 1)])
        g.op(ACT, "activation", (SE[:], snk[:], AF.Exp), reads=[R("snk", 0, 0, 1)], writes=[R("SE", 0, 0, 1)])

        for nm, tl in (("STG", STG), ("KOUT", KOUT), ("VOUT", VOUT), ("VSOUT", VSOUT)):
            g.op(DVE, "memset", (tl[:], 0.0), writes=[R(nm, 0, 0, 1)] + ([R(nm, 1, 0, 1)] if nm == "STG" else []))

        s_old = g.new_sem("old")
        g.dma(SP, o_kold, ckraw[:, 16:128, :], s_old)
        g.dma(SP, o_vold, cvraw[:, 16:128, :], s_old)
        g.out_sems.append(s_old)

        def get_unit(desc):
            u = g.ui
            assert SEQ[u % NU] == desc, (SEQ[u % NU], desc)
            g.ui += 1
            slot = u % NS
            g.dma(POOL, wr[slot][:], wall[u % NU], wsem[slot], writes=[R("W", slot, 0, 1)])
            return slot

        def toggle():
            g.tog ^= 1
            return g.tog

        def rmsnorm(gname, tiles):
            for ti, (c0, c1) in enumerate(tiles):
                n = c1 - c0
                bank = 6 + (ti % 2)
                for k in range(KC):
                    sq = sqt[k % 4]
                    g.op(ACT, "activation", (sq[:, 0:n], xT[:, k, c0:c1], AF.Square),
                         reads=[R("x", k, c0, c1)], writes=[R("sqt", k % 4, 0, n)])
                    g.mm(ps[bank][:, 0:n], onesD[:], sq[:, 0:n], k == 0, k == KC - 1,
                         reads=[R("sqt", k % 4, 0, n), R("onesD", 0, 0, 1)], writes=[R("ps", bank, 0, n)],
                         inc=True)
                r = rs[ti % 2]
                g.op(ACT, "activation", (r[:, 0:n], ps[bank][:, 0:n], AF.Sqrt),
                     {"bias": ccol("eps")}, reads=[R("ps", bank, 0, n), R("cst", 0, 0, 1)],
                     writes=[R("rs", ti % 2, 0, n)])
                g.op(DVE, "reciprocal", (r[:, 0:n], r[:, 0:n]), reads=[R("rs", ti % 2, 0, n)],
                     writes=[R("rs", ti % 2, 0, n)])
                for k in range(KC):
                    g.op(DVE, "scalar_tensor_tensor",
                         (hT[:, k, c0:c1], xT[:, k, c0:c1], ccol(gname, k), r[:, 0:n], ALU.mult, ALU.mult),
                         reads=[R("x", k, c0, c1), R("rs", ti % 2, 0, n)], writes=[R("h", k, c0, c1)])

        def proj(desc, src, srcname, nk, tiles, banks):
            slot = get_unit(desc)
            for k in range(nk):
                for ti, (c0, c1) in enumerate(tiles):
                    n = c1 - c0
                    g.mm(ps[banks[ti]][:, 0:n], wr[slot][:, k, :], src[:, k, c0:c1], k == 0, k == nk - 1,
                         reads=[R("W", slot, 0, 1), R(srcname, k, c0, c1)], writes=[R("ps", banks[ti], 0, n)],
                         inc=(k == nk - 1))

        def ffn(l, tiles):
            rmsnorm(("ffn_norm", l), tiles)
            for p, part in enumerate(PARTS):
                for fl, f in enumerate(part):
                    proj(("wg", l, f), hT, "h", KC, tiles, (0, 1))
                    proj(("wu", l, f), hT, "h", KC, tiles, (2, 3))
                    for ti, (c0, c1) in enumerate(tiles):
                        n = c1 - c0
                        e = ev[ti]
                        g.op(ACT, "activation", (e[:, 0:n], ps[ti][:, 0:n], AF.Silu),
                             reads=[R("ps", ti, 0, n)], writes=[R("ev", ti, 0, n)])
                        g.op(DVE, "tensor_tensor", (b2[:, fl, c0:c1], ps[2 + ti][:, 0:n], e[:, 0:n], ALU.mult),
                             reads=[R("ps", 2 + ti, 0, n), R("ev", ti, 0, n)], writes=[R("b2", fl, c0, c1)])
                for dch in range(KC):
                    t = toggle()
                    banks = (4 + 2 * t, 5 + 2 * t)
                    proj(("wd", l, p, dch), b2, "b2", len(part), tiles, banks)
                    for ti, (c0, c1) in enumerate(tiles):
                        n = c1 - c0
                        g.op(DVE, "tensor_tensor", (xT[:, dch, c0:c1], ps[banks[ti]][:, 0:n], xT[:, dch, c0:c1], ALU.add),
                             reads=[R("ps", banks[ti], 0, n), R("x", dch, c0, c1)], writes=[R("x", dch, c0, c1)])

        def conv_layer(i, pa, tiles):
            W = W0 if pa == 0 else W1
            if pa == 0:
                segs = [(0, W0, 30)]
                L = W0
            else:
                segs = [(0, 512, 30), (512, 528, 572), (528, 544, 618)]
                L = 604

            def split(c0, c1):
                out = []
                for (a0, a1, u0) in segs:
                    lo, hi = max(a0, c0), min(a1, c1)
                    if lo < hi:
                        out.append((lo, hi, u0 + (lo - a0)))
                return out

            rmsnorm(("conv_norm", i), tiles)
            wdw = _cc[("w_dw", i)]
            for j in range(KC):
                ub = j % 2
                U = UT[ub]
                if pa == 0:
                    g.op(DVE, "memset", (U[:, 0:30], 0.0), writes=[R("UT", ub, 0, 30)])
                else:
                    g.op(ACT, "activation", (U[:, 0:30], carry[:, i, j, :], AF.Copy),
                         reads=[R("carry", i * KC + j, 0, 30)], writes=[R("UT", ub, 0, 30)])
                    for s in range(2):
                        u0 = 542 + 46 * s
                        g.op(ACT, "activation", (U[:, u0:u0 + 30], scv[:, i, j, s, :], AF.Copy),
                             reads=[R("scv", 0, 0, 1)], writes=[R("UT", ub, u0, u0 + 30)])
                proj(("pw1", i, j), hT, "h", KC, tiles, (0, 1))
                proj(("pw1", i, 16 + j), hT, "h", KC, tiles, (2, 3))
                for ti, (c0, c1) in enumerate(tiles):
                    n = c1 - c0
                    e = ev[ti]
                    g.op(ACT, "activation", (e[:, 0:n], ps[2 + ti][:, 0:n], AF.Sigmoid),
                         {"bias": ccol(("b_pw1g", i), j)},
                         reads=[R("ps", 2 + ti, 0, n), R("cst", 0, 0, 1)], writes=[R("ev", ti, 0, n)])
                    for (a0, a1, u0) in split(c0, c1):
                        m = a1 - a0
                        g.op(DVE, "scalar_tensor_tensor",
                             (U[:, u0:u0 + m], ps[ti][:, a0 - c0:a1 - c0], ccol(("b_pw1a", i), j),
                              e[:, a0 - c0:a1 - c0], ALU.add, ALU.mult),
                             reads=[R("ps", ti, a0 - c0, a1 - c0), R("ev", ti, a0 - c0, a1 - c0)],
                             writes=[R("UT", ub, u0, u0 + m)])
                if pa == 0:
                    g.op(DVE, "tensor_scalar", (U[:, 30:30 + HALO], U[:, 30:30 + HALO], ccol("halo_mask"), None, ALU.mult),
                         reads=[R("UT", ub, 30, 30 + HALO)], writes=[R("UT", ub, 30, 30 + HALO)])
                    g.op(ACT, "activation", (carry[:, i, j, :], U[:, W0:W0 + 30], AF.Copy),
                         reads=[R("UT", ub, W0, W0 + 30)], writes=[R("carry", i * KC + j, 0, 30)])
                else:
                    g.op(ACT, "activation", (STG[:, i, j, 0, :], U[:, 512:542], AF.Copy),
                         reads=[R("UT", ub, 512, 542)], writes=[R("STG", i, 0, 1)])
                    for s in range(2):
                        u0 = 542 + 46 * s + 16
                        g.op(ACT, "activation", (STG[:, i, j, 1 + s, :], U[:, u0:u0 + 30], AF.Copy),
                             reads=[R("UT", ub, u0, u0 + 30)], writes=[R("STG", i, 0, 1)])
                ct = ctmp[ub]
                g.op(DVE, "tensor_scalar",
                     (ct[:, 0:L], U[:, 0:L], cst[:, wdw + j:wdw + j + 1], ccol(("b_dw", i), j), ALU.mult, ALU.add),
                     reads=[R("UT", ub, 0, L), R("cst", 0, 0, 1)], writes=[R("ctmp", ub, 0, L)])
                for tap in range(1, 31):
                    wc = wdw + tap * 16 + j
                    g.op(DVE, "scalar_tensor_tensor",
                         (ct[:, 0:L], U[:, tap:tap + L], cst[:, wc:wc + 1], ct[:, 0:L], ALU.mult, ALU.add),
                         reads=[R("UT", ub, tap, tap + L), R("ctmp", ub, 0, L)], writes=[R("ctmp", ub, 0, L)])
                for (a0, a1, u0) in segs:
                    ci = u0 - 30
                    g.op(ACT, "activation", (b2[:, j, a0:a1], ct[:, ci:ci + (a1 - a0)], AF.Copy),
                         reads=[R("ctmp", ub, ci, ci + (a1 - a0))], writes=[R("b2", j, a0, a1)])
            for ti, (c0, c1) in enumerate(tiles):
                n = c1 - c0
                bm, bq = 4 + ti, 6 + ti
                for j in range(KC):
                    sq = sqt[j % 4]
                    g.op(ACT, "activation", (sq[:, 0:n], b2[:, j, c0:c1], AF.Square),
                         reads=[R("b2", j, c0, c1)], writes=[R("sqt", j % 4, 0, n)])
                    g.mm(ps[bm][:, 0:n], onesD[:], b2[:, j, c0:c1], j == 0, j == KC - 1,
                         reads=[R("b2", j, c0, c1)], writes=[R("ps", bm, 0, n)], inc=True)
                    g.mm(ps[bq][:, 0:n], onesD[:], sq[:, 0:n], j == 0, j == KC - 1,
                         reads=[R("sqt", j % 4, 0, n)], writes=[R("ps", bq, 0, n)], inc=True)
                mean, rstd, nmr = lnm[ti], lnr[ti], lnn[ti]
                g.op(ACT, "activation", (mean[:, 0:n], ps[bm][:, 0:n], AF.Copy),
                     reads=[R("ps", bm, 0, n)], writes=[R("ytmp", ti, 0, n)])
                g.op(DVE, "tensor_tensor", (nmr[:, 0:n], mean[:, 0:n], mean[:, 0:n], ALU.mult),
                     reads=[R("ytmp", ti, 0, n)], writes=[R("rs", ti, 0, n)])
                g.op(DVE, "tensor_tensor", (rstd[:, 0:n], ps[bq][:, 0:n], nmr[:, 0:n], ALU.subtract),
                     reads=[R("ps", bq, 0, n), R("rs", ti, 0, n)], writes=[R("fin", ti, 0, n)])
                g.op(ACT, "activation", (rstd[:, 0:n], rstd[:, 0:n], AF.Sqrt), {"bias": ccol("eps")},
                     reads=[R("fin", ti, 0, n)], writes=[R("fin", ti, 0, n)])
                g.op(DVE, "reciprocal", (rstd[:, 0:n], rstd[:, 0:n]),
                     reads=[R("fin", ti, 0, n)], writes=[R("fin", ti, 0, n)])
                g.op(DVE, "scalar_tensor_tensor", (nmr[:, 0:n], mean[:, 0:n], -1.0, rstd[:, 0:n], ALU.mult, ALU.mult),
                     reads=[R("ytmp", ti, 0, n), R("fin", ti, 0, n)], writes=[R("rs", ti, 0, n)])
                for j in range(KC):
                    e = ev[j % 4]
                    g.op(DVE, "tensor_tensor", (e[:, 0:n], b2[:, j, c0:c1], rstd[:, 0:n], ALU.mult),
                         reads=[R("b2", j, c0, c1), R("fin", ti, 0, n)], writes=[R("ev", j % 4, 0, n)])
                    g.op(DVE, "tensor_tensor", (e[:, 0:n], e[:, 0:n], nmr[:, 0:n], ALU.add),
                         reads=[R("ev", j % 4, 0, n), R("rs", ti, 0, n)], writes=[R("ev", j % 4, 0, n)])
                    g.op(ACT, "activation", (b2[:, j, c0:c1], e[:, 0:n], AF.Silu),
                         {"bias": ccol(("ln_b", i), j), "scale": ccol(("ln_g", i), j)},
                         reads=[R("ev", j % 4, 0, n), R("cst", 0, 0, 1)], writes=[R("b2", j, c0, c1)])
            for dch in range(KC):
                t = toggle()
                banks = (4 + 2 * t, 5 + 2 * t)
                proj(("pw2", i, dch), b2, "b2", KC, tiles, banks)
                for ti, (c0, c1) in enumerate(tiles):
                    n = c1 - c0
                    g.op(DVE, "scalar_tensor_tensor",
                         (xT[:, dch, c0:c1], ps[banks[ti]][:, 0:n], ccol(("b_pw2", i), dch), xT[:, dch, c0:c1],
                          ALU.add, ALU.add),
                         reads=[R("ps", banks[ti], 0, n), R("x", dch, c0, c1)], writes=[R("x", dch, c0, c1)])

        def head_norm_rope(bank, ti, c0, c1, gcolname):
            n = c1 - c0
            sq = sqt[ti]
            g.op(ACT, "activation", (sq[:, 0:n], ps[bank][:, 0:n], AF.Square),
                 reads=[R("ps", bank, 0, n)], writes=[R("sqt", ti, 0, n)])
            g.mm(ps[ti][:, 0:n], blk64[:], sq[:, 0:n], True, True,
                 reads=[R("sqt", ti, 0, n), R("blk64", 0, 0, 1)], writes=[R("ps", ti, 0, n)], inc=True)
            r = rs[ti]
            g.op(ACT, "activation", (r[:, 0:n], ps[ti][:, 0:n], AF.Sqrt), {"bias": ccol("eps")},
                 reads=[R("ps", ti, 0, n), R("cst", 0, 0, 1)], writes=[R("rs", ti, 0, n)])
            g.op(DVE, "reciprocal", (r[:, 0:n], r[:, 0:n]), reads=[R("rs", ti, 0, n)], writes=[R("rs", ti, 0, n)])
            y = ytmp[ti]
            g.op(DVE, "scalar_tensor_tensor", (y[:, 0:n], ps[bank][:, 0:n], ccol(gcolname), r[:, 0:n], ALU.mult, ALU.mult),
                 reads=[R("ps", bank, 0, n), R("rs", ti, 0, n)], writes=[R("ytmp", ti, 0, n)])
            yb = ybt[ti]
            g.op(ACT, "activation", (yb[:, 0:n], y[:, 0:n], AF.Copy),
                 reads=[R("ytmp", ti, 0, n)], writes=[R("ybt", ti, 0, n)])
            g.mm(ps[2 + ti][:, 0:n], perm[:], yb[:, 0:n], True, True,
                 reads=[R("ybt", ti, 0, n), R("perm", 0, 0, 1)], writes=[R("ps", 2 + ti, 0, n)], inc=True)
            f = fin[ti]
            g.op(DVE, "tensor_tensor", (y[:, 0:n], y[:, 0:n], rope[:, 0, c0:c1], ALU.mult),
                 reads=[R("ytmp", ti, 0, n), R("rope", 0, c0, c1)], writes=[R("ytmp", ti, 0, n)])
            g.op(DVE, "tensor_tensor", (f[:, 0:n], ps[2 + ti][:, 0:n], rope[:, 1, c0:c1], ALU.mult),
                 reads=[R("ps", 2 + ti, 0, n), R("rope", 0, c0, c1)], writes=[R("fin", ti, 0, n)])
            return f, y

        def kv_compute(pa, doK=True, doV=True):
            if pa == 0:
                ktiles = [(64, 384), (384, 704)]
                kbase = -64
                vt = [(64 + 128 * t, t) for t in range(5)]
            else:
                ktiles = [(0, 272), (272, 544)]
                kbase = 640
                vt = [(128 * t, 5 + t) for t in range(4)]
            ntiles = [(0, 352), (352, 704)] if pa == 0 else [(0, 272), (272, 544)]
            rmsnorm("kv_norm", ntiles)
            for ver in range(2 if doK else 0):
                for i in range(2):
                    t = toggle()
                    banks = (4 + 2 * t, 5 + 2 * t)
                    proj(("wk", ver, i), hT, "h", KC, ktiles, banks)
                    for ti, (c0, c1) in enumerate(ktiles):
                        n = c1 - c0
                        f, y = head_norm_rope(banks[ti], ti, c0, c1, "k_norm")
                        g.op(DVE, "tensor_tensor", (f[:, 0:n], f[:, 0:n], y[:, 0:n], ALU.add),
                             reads=[R("fin", ti, 0, n), R("ytmp", ti, 0, n)], writes=[R("fin", ti, 0, n)])
                        hl, hh = (2 * i, 2 * i + 1) if ver == 0 else (2 * i + 1, 2 * i)
                        pc1 = min(c1, 512) if pa == 1 else c1
                        if pc1 > c0:
                            m = pc1 - c0
                            k0 = c0 + kbase
                            g.op(ACT, "activation", (KT[0:64, hl, k0:k0 + m], f[0:64, 0:m], AF.Copy),
                                 reads=[R("fin", ti, 0, m)], writes=[R("KT", hl, k0, k0 + m)])
                            g.op(ACT, "activation", (KT[64:128, hh, k0:k0 + m], f[64:128, 0:m], AF.Copy),
                                 reads=[R("fin", ti, 0, m)], writes=[R("KT", hh, k0, k0 + m)])
                        if pa == 1 and c1 > 512:
                            o = 512 - c0
                            g.op(ACT, "activation",
                                 (KTs[0:64, hl, :, 128:144], f[0:64, o:o + 32].rearrange("p (s t) -> p s t", s=2), AF.Copy),
                                 reads=[R("fin", ti, o, o + 32)], writes=[R("KTs", 0, 128, 144)])
                            g.op(ACT, "activation",
                                 (KTs[64:128, hh, :, 128:144], f[64:128, o:o + 32].rearrange("p (s t) -> p s t", s=2), AF.Copy),
                                 reads=[R("fin", ti, o, o + 32)], writes=[R("KTs", 0, 128, 144)])
                            if ver == 0:
                                o2 = 384 - c0
                                g.op(ACT, "activation", (KOUT[:, i, :], f[:, o2:o2 + 160], AF.Copy),
                                     reads=[R("fin", ti, o2, o2 + 160)], writes=[R("KOUT", 0, 0, 1)])
            for half in range(2 if doV else 0):
                slot = get_unit(("wv", half))
                for (a, t) in vt:
                    tg = toggle()
                    bank = 4 + 2 * tg
                    for k in range(KC):
                        g.mm(ps[bank][:, 0:128], hT[:, k, a:a + 128], wr[slot][:, k, :], k == 0, k == KC - 1,
                             reads=[R("W", slot, 0, 1), R("h", k, a, a + 128)], writes=[R("ps", bank, 0, 128)],
                             inc=(k == KC - 1))
                    for gg in range(2):
                        gi = 2 * half + gg
                        g.op(ACT, "activation", (Vd[:, t, gi, 0:64], ps[bank][:, gg * 64:(gg + 1) * 64], AF.Copy),
                             reads=[R("ps", bank, 0, 128)], writes=[R("Vd", t * 4 + gi, 0, 1)])
                        g.op(DVE, "tensor_copy", (Vd[:, t, gi, 64:128], ps[bank][:, gg * 64:(gg + 1) * 64]),
                             reads=[R("ps", bank, 0, 128)], writes=[R("Vd", t * 4 + gi, 0, 1)])
                    if pa == 1 and t == 8:
                        g.op(DVE, "tensor_copy", (VOUT[:, half * 128:(half + 1) * 128], ps[bank][:, 0:128]),
                             reads=[R("ps", bank, 0, 128)], writes=[R("VOUT", 0, 0, 1)])
                if pa == 1:
                    for s in range(2):
                        tg = toggle()
                        bank = 4 + 2 * tg
                        a = 512 + 16 * s
                        for k in range(KC):
                            g.mm(ps[bank][0:16, 0:128], hT[:, k, a:a + 16], wr[slot][:, k, :], k == 0, k == KC - 1,
                                 reads=[R("W", slot, 0, 1), R("h", k, a, a + 16)], writes=[R("ps", bank, 0, 128)],
                                 inc=(k == KC - 1))
                        for gg in range(2):
                            gi = 2 * half + gg
                            g.op(ACT, "activation", (Vn[0:16, s, gi, 0:64], ps[bank][0:16, gg * 64:(gg + 1) * 64], AF.Copy),
                                 reads=[R("ps", bank, 0, 128)], writes=[R("Vn", 0, 0, 1)])
                            g.op(DVE, "tensor_copy", (Vn[0:16, s, gi, 64:128], ps[bank][0:16, gg * 64:(gg + 1) * 64]),
                                 reads=[R("ps", bank, 0, 128)], writes=[R("Vn", 0, 0, 1)])
                        g.op(DVE, "tensor_copy", (VSOUT[0:16, s, half * 128:(half + 1) * 128], ps[bank][0:16, 0:128]),
                             reads=[R("ps", bank, 0, 128)], writes=[R("VSOUT", 0, 0, 1)])

        def attn_block(jb, grp, q0, nq, ktl):
            t = toggle()
            ob, db = (4, 5) if t == 0 else (6, 7)
            N4 = 4 * nq
            N8 = 8 * nq
            pts = PT[t]
            for par in range(2):
                ph = slice(0, 64) if par == 0 else slice(64, 128)
                for ki, kd in enumerate(ktl):
                    M = kd["M"]
                    g.mm(ps[par * 2 + ki][0:M, 0:N4], kd["kt"](ph),
                         b2[ph, 4 * grp:4 * grp + 4, q0:q0 + nq], True, True, tp=((0, 0) if par == 0 else None),
                         reads=kd["kreads"] + [R("b2", 4 * grp + ii, q0, q0 + nq) for ii in range(4)],
                         writes=[R("ps", par * 2 + ki, 0, N4)], inc=True)
            for ki, kd in enumerate(ktl):
                r0, r1 = kd["r0"], kd["r1"]
                kw = {"scale": 0.125}
                bias_ap = kd["bias"] if kd["bias"] is not None else cst[:, _cc["zero"]:_cc["zero"] + 1]
                kw["bias"] = bias_ap[r0:r1, :]
                if r0 == 0 and r1 < 128:
                    g.op(DVE, "memset", (pts[ki][:, 0:N8], 0.0), writes=[R("PT", t * 2 + ki, 0, N8)])
                for par in range(2):
                    g.op(ACT, "activation", (pts[ki][r0:r1, par * N4:(par + 1) * N4], ps[par * 2 + ki][r0:r1, 0:N4], AF.Exp), kw,
                         reads=[R("ps", par * 2 + ki, 0, N4), R("cst", 0, 0, 1)],
                         writes=[R("PT", t * 2 + ki, par * N4, (par + 1) * N4)])
            nk = len(ktl)
            for ki, kd in enumerate(ktl):
                r0, r1 = kd["r0"], kd["r1"]
                tpv = None
                vap = kd["v"]
                if r0 == 0 and r1 < 128:
                    r1 = 128
                    vap = kd["vfull"]
                g.mm(ps[ob][:, 0:N8], vap, pts[ki][r0:r1, 0:N8], ki == 0, ki == nk - 1, tp=tpv,
                     reads=kd["vreads"] + [R("PT", t * 2 + ki, 0, N8)], writes=[R("ps", ob, 0, N8)], inc=False)
                g.mm(ps[db][:, 0:N8], ones1[r0:r1, :], pts[ki][r0:r1, 0:N8], ki == 0, ki == nk - 1, tp=tpv,
                     reads=[R("PT", t * 2 + ki, 0, N8), R("ones1", 0, 0, 1)], writes=[R("ps", db, 0, N8)],
                     inc=(ki == nk - 1))
            dn = den[t]
            sbase = jb * 32 + grp * 8
            g.op(DVE, "tensor_copy", (dn[:, 0:N8], ps[db][:, 0:N8]),
                 reads=[R("ps", db, 0, N8)], writes=[R("den", t, 0, N8)])
            g.op(DVE, "tensor_tensor",
                 (dn[:, 0:N8].rearrange("p (a b) -> p a b", a=8), dn[:, 0:N8].rearrange("p (a b) -> p a b", a=8),
                  SE[:, sbase:sbase + 8].unsqueeze(2).to_broadcast([128, 8, nq]), ALU.add),
                 reads=[R("den", t, 0, N8), R("SE", 0, 0, 1)], writes=[R("den", t, 0, N8)])
            g.op(DVE, "reciprocal", (dn[:, 0:N8], dn[:, 0:N8]), reads=[R("den", t, 0, N8)], writes=[R("den", t, 0, N8)])
            for par in range(2):
                ph = slice(0, 64) if par == 0 else slice(64, 128)
                for ii in range(4):
                    cs = par * N4 + ii * nq
                    g.op(DVE, "tensor_tensor",
                         (hT[ph, 4 * grp + ii, q0:q0 + nq], ps[ob][ph, cs:cs + nq], dn[ph, cs:cs + nq], ALU.mult),
                         reads=[R("ps", ob, 0, N8), R("den", t, 0, N8)],
                         writes=[R("h", 4 * grp + ii, q0, q0 + nq)])

        def attn_layer(jb, pa, tiles):
            l = 2 + jb
            rmsnorm(("attn_norm", jb), tiles)
            for j in range(KC):
                t = toggle()
                banks = (4 + 2 * t, 5 + 2 * t)
                proj(("wq", jb, j), hT, "h", KC, tiles, banks)
                for ti, (c0, c1) in enumerate(tiles):
                    n = c1 - c0
                    f, y = head_norm_rope(banks[ti], ti, c0, c1, ("q_norm", jb))
                    g.op(DVE, "tensor_tensor", (b2[:, j, c0:c1], f[:, 0:n], y[:, 0:n], ALU.add),
                         reads=[R("fin", ti, 0, n), R("ytmp", ti, 0, n)], writes=[R("b2", j, c0, c1)])
            hb = cst[:, _cc["halo_bias"]:_cc["halo_bias"] + 1]
            for cc in range(8):
                cg = cc + 8 * pa
                q0 = (HALO if pa == 0 else 0) + 64 * cc
                t0 = cg // 2
                for grp in range(4):
                    def mk(tile, r0, r1, bias, grp=grp):
                        return dict(kt=(lambda ph, tile=tile, grp=grp: KT[ph, grp, 128 * tile:128 * tile + 128]), M=128,
                                    r0=r0, r1=r1, v=Vd[r0:r1, tile, grp, :], vfull=Vd[:, tile, grp, :], bias=bias,
                                    kreads=[R("KT", grp, 128 * tile, 128 * tile + 128)],
                                    vreads=[R("Vd", tile * 4 + grp, 0, 1)])
                    if cg % 2 == 0:
                        ktl = [mk(t0, 0, 128, hb if t0 == 0 else None), mk(t0 + 1, 0, 64, None)]
                    else:
                        ktl = [mk(t0, 64, 128, hb if t0 == 0 else None), mk(t0 + 1, 0, 128, None)]
                    attn_block(jb, grp, q0, 64, ktl)
            if pa == 1:
                for s in range(2):
                    q0 = 512 + 16 * s
                    for grp in range(4):
                        ktl = [dict(kt=(lambda ph, s=s, grp=grp: KTs[ph, grp, s, 0:128]), M=128, r0=0, r1=128,
                                    v=Vds[:, s, grp, :], vfull=Vds[:, s, grp, :], bias=None, kreads=[R("KTs", 0, 0, 128)], vreads=[R("Vds", 0, 0, 1)]),
                               dict(kt=(lambda ph, s=s, grp=grp: KTs[ph, grp, s, 128:144]), M=16, r0=0, r1=16,
                                    v=Vn[0:16, s, grp, :], vfull=Vn[:, s, grp, :], bias=None, kreads=[R("KTs", 0, 128, 144)], vreads=[R("Vn", 0, 0, 1)])]
                        attn_block(jb, grp, q0, 16, ktl)
            for dch in range(KC):
                t = toggle()
                banks = (4 + 2 * t, 5 + 2 * t)
                proj(("wo", jb, dch), hT, "h", KC, tiles, banks)
                for ti, (c0, c1) in enumerate(tiles):
                    n = c1 - c0
                    g.op(DVE, "tensor_tensor", (xT[:, dch, c0:c1], ps[banks[ti]][:, 0:n], xT[:, dch, c0:c1], ALU.add),
                         reads=[R("ps", banks[ti], 0, n), R("x", dch, c0, c1)], writes=[R("x", dch, c0, c1)])
            if "noffn" not in stages:
                ffn(l, tiles)

        for pa in passes:
            W = W0 if pa == 0 else W1
            g.dma(SP, xT[:, :, 0:W], xin[pa], s_x, writes=[R("x", k, 0, W) for k in range(KC)])
            g.dma(SP, rope[:, :, 0:W], ropein[pa], s_rope, writes=[R("rope", 0, 0, W)])
            tA = [(0, 352), (352, 704)] if pa == 0 else [(0, 272), (272, 544)]
            tB = [(192, 448), (448, 704)] if pa == 0 else [(0, 272), (272, 544)]
            for i in range(2):
                if "conv%d" % i in stages:
                    conv_layer(i, pa, tA)
                if "ffn%d" % i in stages:
                    ffn(i, tA)
            if "kv" in stages or "kvk" in stages or "kvv" in stages:
                kv_compute(pa, "kv" in stages or "kvk" in stages, "kv" in stages or "kvv" in stages)
            for jb in range(2):
                if "attn%d" % jb in stages:
                    attn_layer(jb, pa, tB)
            if "dbg" in stages and pa == 0:
                sd = g.new_sem("dbgs")
                tl_ = g.tog
                ca, cb = ctmp[0], ctmp[1]
                g.op(ACT, "activation", (ca[:, 0:256].rearrange("p (a b) -> p a b", a=4), b2[:, 12:16, 640:704], AF.Copy),
                     reads=[R("b2", kk, 640, 704) for kk in range(12, 16)], writes=[R("ctmp", 0, 0, 704)])
                g.op(ACT, "activation", (ca[:, 256:512].rearrange("p (a b) -> p a b", a=4), hT[:, 12:16, 640:704], AF.Copy),
                     reads=[R("h", kk, 640, 704) for kk in range(12, 16)], writes=[R("ctmp", 0, 0, 704)])
                g.op(ACT, "activation", (ca[:, 512:704], KT[:, 3, 448:640], AF.Copy),
                     reads=[R("KT", 3, 0, 1152)], writes=[R("ctmp", 0, 0, 704)])
                g.op(ACT, "activation", (cb[:, 0:352], PT[tl_][0][:, 0:352], AF.Copy),
                     reads=[R("PT", tl_ * 2, 0, 512)], writes=[R("ctmp", 1, 0, 704)])
                g.op(ACT, "activation", (cb[:, 352:704], PT[tl_][1][:, 0:352], AF.Copy),
                     reads=[R("PT", tl_ * 2 + 1, 0, 512)], writes=[R("ctmp", 1, 0, 704)])
                g.dma(SP, dbg_out["o_dbg_a"], ca[:], sd, reads=[R("ctmp", 0, 0, 704)])
                g.dma(SP, dbg_out["o_dbg_b"], cb[:], sd, reads=[R("ctmp", 1, 0, 704)])
                g.dma(SP, dbg_out["o_dbg_den0"], den[0][:], sd, reads=[R("den", 0, 0, 512)])
                g.dma(SP, dbg_out["o_dbg_den1"], den[1][:], sd, reads=[R("den", 1, 0, 512)])
                g.out_sems.append(sd)
                print("DBG last toggle", tl_)
            sy = g.new_sem("y%d" % pa)
            if pa == 0:
                g.dma(SP, o_y[0], xT[:, :, HALO:W0], sy, reads=[R("x", k, HALO, W0) for k in range(KC)])
            else:
                g.dma(SP, o_y[1], xT[:, :, 0:W1], sy, reads=[R("x", k, 0, W1) for k in range(KC)])
            g.out_sems.append(sy)
        assert g.ui == len(passes) * NU
        so = g.new_sem("outs")
        so2 = g.new_sem("outs2")
        g.dma(POOL, o_st, STG[:], so2, reads=[R("STG", 0, 0, 1), R("STG", 1, 0, 1)])
        g.out_sems.append(so2)
        g.dma(SP, o_k, KOUT[:], so, reads=[R("KOUT", 0, 0, 1)])
        g.dma(SP, o_v, VOUT[:], so, reads=[R("VOUT", 0, 0, 1)])
        g.dma(SP, o_vs, VSOUT[:], so, reads=[R("VSOUT", 0, 0, 1)])
        g.out_sems.append(so)
        for s in g.out_sems:
            SP.ops.append(("wait", s, s.count))

        for s in g.sems:
            s.h = es.enter_context(nc.semaphore(s.name))
        block = es.enter_context(nc.Block())

        def emitter(st):
            def run(e):
                for o in st.ops:
                    if o[0] == "wait":
                        e.wait_ge(o[1].h, o[2])
                    else:
                        _, method, args, kw, sem, inc = o
                        ins = getattr(e, method)(*args, **kw)
                        if sem is not None:
                            ins.then_inc(sem.h, inc)
            return run

        block.tensor(emitter(PE))
        block.scalar(emitter(ACT))
        block.vector(emitter(DVE))
        block.gpsimd(emitter(POOL))
        block.sync(emitter(SP))
    g.stats = {k: len(getattr(g, k).ops) for k in ("PE", "ACT", "DVE", "POOL", "SP")}
    _check_deadlock([g.PE, g.ACT, g.DVE, g.POOL, g.SP])
    return nc, g


def _fm(a):
    T = a.shape[0]
    return np.ascontiguousarray(a.T.reshape(KC, 128, T).transpose(1, 0, 2))


def _unfm(a):
    p, k, T = a.shape
    return np.ascontiguousarray(a.transpose(2, 1, 0).reshape(T, k * p))


def _colvec(v):
    return np.ascontiguousarray(np.asarray(v, np.float32).reshape(KC, 128).T)


def _pack_wall(inp):
    wall = np.zeros((NU, 128, KC, 128), np.float32)

    def pk(Wm, r0, nk, cols):
        blk = Wm[r0:r0 + nk * 128][:, cols]
        return blk.reshape(nk, 128, 128).transpose(1, 0, 2)

    for u, d in enumerate(SEQ):
        kind = d[0]
        if kind == "pw1":
            wall[u] = pk(inp["w_pw1"][d[1]], 0, KC, slice(d[2] * 128, d[2] * 128 + 128))
        elif kind == "pw2":
            wall[u] = pk(inp["w_pw2"][d[1]], 0, KC, slice(d[2] * 128, d[2] * 128 + 128))
        elif kind == "wg":
            wall[u] = pk(inp["w_gate"][d[1]], 0, KC, slice(d[2] * 128, d[2] * 128 + 128))
        elif kind == "wu":
            wall[u] = pk(inp["w_up"][d[1]], 0, KC, slice(d[2] * 128, d[2] * 128 + 128))
        elif kind == "wd":
            part = PARTS[d[2]]
            wall[u, :, 0:len(part), :] = pk(inp["w_down"][d[1]], part[0] * 128, len(part), slice(d[3] * 128, d[3] * 128 + 128))
        elif kind == "wk":
            ver, i = d[1], d[2]
            if ver == 0:
                cols = np.arange(i * 128, i * 128 + 128)
            else:
                cols = np.concatenate([np.arange((2 * i + 1) * 64, (2 * i + 2) * 64), np.arange(2 * i * 64, (2 * i + 1) * 64)])
            wall[u] = pk(inp["w_k"], 0, KC, cols)
        elif kind == "wv":
            wall[u] = pk(inp["w_v"], 0, KC, slice(d[1] * 128, d[1] * 128 + 128))
        elif kind == "wq":
            wall[u] = pk(inp["w_q"][d[1]], 0, KC, slice(d[2] * 128, d[2] * 128 + 128))
        elif kind == "wo":
            wall[u] = pk(inp["w_o"][d[1]], 0, KC, slice(d[2] * 128, d[2] * 128 + 128))
        else:
            raise ValueError(d)
    return wall


def _rope_table(pos):
    half = 32
    inv_freq = (10000.0 ** (-np.arange(half, dtype=np.float32) / half)).astype(np.float32)
    ang = pos.astype(np.float32)[:, None] * inv_freq[None, :]
    cos = np.cos(ang).astype(np.float32)
    sin = np.sin(ang).astype(np.float32)
    tab = np.zeros((128, 2, len(pos)), np.float32)
    for p in range(128):
        d = p % 64
        tab[p, 0] = cos[:, d % 32]
        tab[p, 1] = -sin[:, d % 32] if d < 32 else sin[:, d % 32]
    return tab


_CACHE = {}


def kernel(_plan=None, **inp):
    inp = {k: np.asarray(v) for k, v in inp.items()}
    if _plan is not None:
        nc, g = build_program(*_plan)
    else:
        if "nc" not in _CACHE:
            _CACHE["nc"] = build_program()
        nc, g = _CACHE["nc"]

    wall = _pack_wall(inp)
    xp = inp["x_prompt"][0]
    xs = inp["x_sample"]
    perm = np.zeros((128, 128), np.float32)
    for m in range(128):
        d = m % 64
        perm[(m - d) + ((d + 32) % 64), m] = 1.0

    cbase = np.zeros((128, NCONST), np.float32)
    for i in range(2):
        cbase[:, _cc[("conv_norm", i)]:][:, :16] = _colvec(inp["conv_norm"][i])
        cbase[:, _cc[("b_pw1a", i)]:][:, :16] = _colvec(inp["b_pw1"][i][:D])
        cbase[:, _cc[("b_pw1g", i)]:][:, :16] = _colvec(inp["b_pw1"][i][D:])
        for tap in range(31):
            cbase[:, _cc[("w_dw", i)] + tap * 16:][:, :16] = _colvec(inp["w_dw"][i][tap])
        cbase[:, _cc[("b_dw", i)]:][:, :16] = _colvec(inp["b_dw"][i])
        cbase[:, _cc[("ln_g", i)]:][:, :16] = _colvec(inp["conv_ln_g"][i])
        cbase[:, _cc[("ln_b", i)]:][:, :16] = _colvec(inp["conv_ln_b"][i])
        cbase[:, _cc[("b_pw2", i)]:][:, :16] = _colvec(inp["b_pw2"][i])
    for l in range(4):
        cbase[:, _cc[("ffn_norm", l)]:][:, :16] = _colvec(inp["ffn_norm"][l])
    cbase[:, _cc["kv_norm"]:][:, :16] = _colvec(inp["kv_norm"])
    for j in range(2):
        cbase[:, _cc[("attn_norm", j)]:][:, :16] = _colvec(inp["attn_norm"][j])
        cbase[:, _cc[("q_norm", j)]] = np.tile(inp["q_norm"][j], 2)
    cbase[:, _cc["k_norm"]] = np.tile(inp["k_norm"], 2)
    cbase[:, _cc["eps"]] = EPS

    sinks = np.zeros((128, 64), np.float32)
    for jb in range(2):
        for grp in range(4):
            for par in range(2):
                for i in range(4):
                    sinks[:, jb * 32 + grp * 8 + par * 4 + i] = inp["sinks"][jb, 8 * grp + 2 * i + par]

    in_maps = []
    for c in range(NCORE):
        base = 1024 * c
        x0 = np.zeros((W0, D), np.float32)
        lo = base - HALO
        if lo < 0:
            x0[-lo:] = xp[0:base + OWN]
        else:
            x0[:] = xp[lo:base + OWN]
        x1 = np.concatenate([xp[base + OWN:base + 1024], xs[2 * c], xs[2 * c + 1]], axis=0)
        pos0 = np.arange(lo, base + OWN)
        pos1 = np.concatenate([np.arange(base + OWN, base + 1024), PAST_LEN + np.arange(16), PAST_LEN + np.arange(16)])
        cc = cbase.copy()
        cc[:, _cc["halo_mask"]] = 0.0 if c == 0 else 1.0
        cc[:, _cc["halo_bias"]] = -30000.0 if c == 0 else 0.0
        scv = np.zeros((128, 2, KC, 2, 30), np.float32)
        for i in range(2):
            for s in range(2):
                scv[:, i, :, s, :] = _fm(inp["state_conv"][i, 2 * c + s])
        ckT = np.zeros((128, 4, 2, 128), np.float32)
        cvd = np.zeros((128, 2, 4, 128), np.float32)
        for s in range(2):
            ck = inp["cache_k"][2 * c + s]
            cv = inp["cache_v"][2 * c + s]
            kt = ck.transpose(2, 1, 0)
            ckT[0:64, :, s, :] = kt
            ckT[64:128, :, s, :] = kt
            cvd[:, s, :, 0:64] = cv
            cvd[:, s, :, 64:128] = cv
        in_maps.append({
            "xin0": _fm(x0), "xin1": _fm(x1),
            "rope0": _rope_table(pos0), "rope1": _rope_table(pos1),
            "consts": cc, "sinks": sinks, "perm": perm, "scv": scv, "ckT": ckT, "cvd": cvd,
            "ck_raw": np.ascontiguousarray(inp["cache_k"][2 * c:2 * c + 2].reshape(2, 128, 256)),
            "cv_raw": np.ascontiguousarray(inp["cache_v"][2 * c:2 * c + 2].reshape(2, 128, 256)),
            "wall": wall,
        })

    res = run_bass_kernel_spmd(nc, in_maps, core_ids=list(range(NCORE)))
    outs = res.results
    if _plan is not None:
        return outs

    y_prompt = np.zeros((1, 8192, D), np.float32)
    y_sample = np.zeros((16, 16, D), np.float32)
    st_p = np.zeros((2, 1, 30, D), np.float32)
    st_s = np.zeros((2, 16, 30, D), np.float32)
    k_p = np.zeros((1, 128, 4, 64), np.float32)
    v_p = np.zeros((1, 128, 4, 64), np.float32)
    k_s = np.zeros((16, 128, 4, 64), np.float32)
    v_s = np.zeros((16, 128, 4, 64), np.float32)
    for c in range(NCORE):
        o = outs[c]
        base = 1024 * c
        y_prompt[0, base:base + OWN] = _unfm(o["o_y0"])
        y1 = o["o_y1"]
        y_prompt[0, base + OWN:base + 1024] = _unfm(y1[:, :, 0:OWN])
        for s in range(2):
            b = 2 * c + s
            y_sample[b] = _unfm(y1[:, :, OWN + 16 * s:OWN + 16 * s + 16])
            for i in range(2):
                st_s[i, b] = _unfm(o["o_st"][:, i, :, 1 + s, :])
            knew = _unfm(o["o_k"][:, :, 128 + 16 * s:128 + 16 * s + 16])
            k_s[b] = np.concatenate([o["o_kold"][s], knew], axis=0).reshape(128, 4, 64)
            v_s[b] = np.concatenate([o["o_vold"][s], o["o_vs"][:, s, :]], axis=0).reshape(128, 4, 64)
        if c == NCORE - 1:
            for i in range(2):
                st_p[i, 0] = _unfm(o["o_st"][:, i, :, 0, :])
            k_p[0] = _unfm(o["o_k"][:, :, 0:128]).reshape(128, 4, 64)
            v_p[0] = o["o_v"].reshape(128, 4, 64)
    return (y_prompt, y_sample, st_p, k_p, v_p, st_s, k_s, v_s)
```
